# Optimizing a Trainium2 kernel written in Bass

```python
import jax, jax.numpy as jnp
from jax import lax
import numpy as np

D_MODEL = 1024
BATCH = 8
SEQ = 4096
DEPTH = 2
DEC_BATCH = 32
DEC_SEQ = 16
PAST_LEN = 2048

CHUNK = 64
N_EVEN = (DEPTH + 1) // 2
N_ODD = DEPTH // 2
GMLP_CHUNK = 128
A_GROUP_DIM = 128
D_A = D_MODEL // 2
A_GROUPS = D_A // A_GROUP_DIM
D_B = D_MODEL // 2
CONV_W = 3
C_HEAD_DIM = 64
C_HEADS = D_MODEL // C_HEAD_DIM
D_C = C_HEADS * C_HEAD_DIM
Q_BLOCK = 128
D_FF = 11 * D_MODEL // 4
EPS = 1e-6
NEG_INF = -1e30

kernel_name = "chunk_streaming_gmlp_conv_fox_trunk"


def _rmsnorm(x, g):
    xf = x.astype(jnp.float32)
    y = xf * lax.rsqrt(jnp.mean(xf * xf, axis=-1, keepdims=True) + EPS)
    return (y * g.astype(jnp.float32)).astype(x.dtype)


def _causal_dwconv(x, w, prev):
    t = x.shape[1]
    xp = jnp.concatenate([prev.astype(x.dtype), x], axis=1)
    y = xp[:, 0:t] * w[0]
    for i in range(1, CONV_W):
        y = y + xp[:, i:i + t] * w[i]
    return y, xp[:, t:]


def _chunk_mask(n):
    pos = jnp.arange(n)
    return (pos[None, :] // CHUNK) <= (pos[:, None] // CHUNK)


def _mixer_ab(x, conv_prev, w_in, v_norm, w_s, b_s, conv_w, w_out):
    b, t, _ = x.shape
    z = x @ w_in
    u, v, g_b, g_c, x_b = jnp.split(z, [D_A, 2 * D_A, 2 * D_A + D_B, 2 * D_A + 2 * D_B], axis=-1)
    u = jax.nn.gelu(u)
    v = _rmsnorm(jax.nn.gelu(v).reshape(b, t, A_GROUPS, A_GROUP_DIM), v_norm)
    n = -(-t // GMLP_CHUNK)
    vp = jnp.pad(v, ((0, 0), (0, n * GMLP_CHUNK - t), (0, 0), (0, 0)))
    vp = vp.reshape(b, n, GMLP_CHUNK, A_GROUPS, A_GROUP_DIM)
    w_m = jnp.where(_chunk_mask(GMLP_CHUNK)[None], w_s, 0.0)
    mixed = jnp.einsum('gts,bnsgc->bntgc', w_m, vp) + b_s.T[:, :, None]
    mixed = mixed.reshape(b, n * GMLP_CHUNK, D_A)[:, :t]
    y_a = u * mixed
    conv_out, conv_state = _causal_dwconv(g_c * x_b, conv_w, conv_prev)
    y_b = g_b * conv_out
    y = jnp.concatenate([y_a, y_b], axis=-1) @ w_out
    return y, conv_state, v.reshape(b, t, D_A)


def _fox_project(x, w_in, b_f, q_norm, k_norm):
    b, t, _ = x.shape
    z = x @ w_in
    q, k, v, f = jnp.split(z, [D_C, 2 * D_C, 3 * D_C], axis=-1)
    q = _rmsnorm(q.reshape(b, t, C_HEADS, C_HEAD_DIM), q_norm)
    k = _rmsnorm(k.reshape(b, t, C_HEADS, C_HEAD_DIM), k_norm)
    v = v.reshape(b, t, C_HEADS, C_HEAD_DIM)
    logf = jax.nn.log_sigmoid((f + b_f).astype(jnp.float32))
    return q, k, v, logf


def _fox_attend(q, k, v, dq, dk, q_pos, k_pos):
    s = jnp.einsum('bthd,bshd->bhts', q, k, preferred_element_type=jnp.float32) * (C_HEAD_DIM ** -0.5)
    s = s + jnp.swapaxes(dq, 1, 2)[..., :, None] - jnp.swapaxes(dk, 1, 2)[..., None, :]
    s = jnp.where(k_pos[None, :] <= q_pos[:, None], s, NEG_INF)
    p = jax.nn.softmax(s, axis=-1)
    return jnp.einsum('bhts,bshd->bthd', p.astype(v.dtype), v)


def _mixer_c_prompt(x, w_in, b_f, q_norm, k_norm, w_out):
    b, t, _ = x.shape
    q, k, v, logf = _fox_project(x, w_in, b_f, q_norm, k_norm)
    d = jnp.cumsum(logf, axis=1)
    pos = jnp.arange(t)
    nb = t // Q_BLOCK
    qb = jnp.swapaxes(q.reshape(b, nb, Q_BLOCK, C_HEADS, C_HEAD_DIM), 0, 1)
    db = jnp.swapaxes(d.reshape(b, nb, Q_BLOCK, C_HEADS), 0, 1)
    pb = pos.reshape(nb, Q_BLOCK)
    o = lax.map(lambda blk: _fox_attend(blk[0], k, v, blk[1], d, blk[2], pos), (qb, db, pb))
    o = jnp.swapaxes(o, 0, 1).reshape(b, t, D_C)
    return o @ w_out, k, v, logf


def _mixer_c_sample(x, cache_k, cache_v, cache_logf, w_in, b_f, q_norm, k_norm, w_out):
    b, t, _ = x.shape
    p_len = cache_k.shape[1]
    q, k, v, logf = _fox_project(x, w_in, b_f, q_norm, k_norm)
    c_cum = jnp.cumsum(cache_logf.astype(jnp.float32), axis=1)
    d_new = jnp.cumsum(logf, axis=1)
    dk = jnp.concatenate([c_cum - c_cum[:, -1:], d_new], axis=1)
    keys = jnp.concatenate([cache_k.astype(k.dtype), k], axis=1)
    vals = jnp.concatenate([cache_v.astype(v.dtype), v], axis=1)
    o = _fox_attend(q, keys, vals, d_new, dk, p_len + jnp.arange(t), jnp.arange(p_len + t))
    return o.reshape(b, t, D_C) @ w_out, k, v, logf


def _conv_ffn(x, prev, w_up, conv_w, w_down):
    z, st = _causal_dwconv(x @ w_up, conv_w, prev)
    gate, up = jnp.split(z, 2, axis=-1)
    return (jax.nn.silu(gate) * up) @ w_down, st


def setup_inputs(seed: int = 0) -> dict:
    key = jax.random.key(seed)
    ks = jax.random.split(key, 24)
    f32 = jnp.float32

    def nrm(k, shape, scale):
        return jax.random.normal(k, shape, f32) * scale

    return {
        "x_prompt": nrm(ks[0], (BATCH, SEQ, D_MODEL), 1.0),
        "x_sample": nrm(ks[1], (DEC_BATCH, DEC_SEQ, D_MODEL), 1.0),
        "state_conv_b": nrm(ks[2], (N_EVEN, DEC_BATCH, CONV_W - 1, D_B), 1.0),
        "state_ffn": nrm(ks[3], (DEPTH, DEC_BATCH, CONV_W - 1, 2 * D_FF), 1.0),
        "cache_k": nrm(ks[4], (N_ODD, DEC_BATCH, PAST_LEN, C_HEADS, C_HEAD_DIM), 1.0),
        "cache_v": nrm(ks[5], (N_ODD, DEC_BATCH, PAST_LEN, C_HEADS, C_HEAD_DIM), 1.0),
        "cache_logf": jax.nn.log_sigmoid(jax.random.uniform(ks[6], (N_ODD, DEC_BATCH, PAST_LEN, C_HEADS), f32, 1.0, 6.0)),
        "norm_mix": 1.0 + nrm(ks[7], (DEPTH, D_MODEL), 0.05),
        "norm_ffn": 1.0 + nrm(ks[8], (DEPTH, D_MODEL), 0.05),
        "w_in_ab": nrm(ks[9], (N_EVEN, D_MODEL, 2 * D_A + 3 * D_B), D_MODEL ** -0.5),
        "sgu_norm": 1.0 + nrm(ks[10], (N_EVEN, A_GROUPS, A_GROUP_DIM), 0.05),
        "w_spatial": nrm(ks[11], (N_EVEN, A_GROUPS, GMLP_CHUNK, GMLP_CHUNK), GMLP_CHUNK ** -0.5),
        "b_spatial": 1.0 + nrm(ks[12], (N_EVEN, A_GROUPS, GMLP_CHUNK), 0.05),
        "conv_b": nrm(ks[13], (N_EVEN, CONV_W, D_B), CONV_W ** -0.5),
        "w_out_ab": nrm(ks[14], (N_EVEN, D_A + D_B, D_MODEL), (D_A + D_B) ** -0.5),
        "w_in_c": nrm(ks[15], (N_ODD, D_MODEL, 3 * D_C + C_HEADS), D_MODEL ** -0.5),
        "b_forget": jax.random.uniform(ks[16], (N_ODD, C_HEADS), f32, 1.0, 6.0),
        "q_norm": 1.0 + nrm(ks[17], (N_ODD, C_HEAD_DIM), 0.05),
        "k_norm": 1.0 + nrm(ks[18], (N_ODD, C_HEAD_DIM), 0.05),
        "w_out_c": nrm(ks[19], (N_ODD, D_C, D_MODEL), D_C ** -0.5),
        "w_up": nrm(ks[20], (DEPTH, D_MODEL, 2 * D_FF), D_MODEL ** -0.5),
        "conv_ffn": nrm(ks[21], (DEPTH, CONV_W, 2 * D_FF), CONV_W ** -0.5),
        "w_down": nrm(ks[22], (DEPTH, D_FF, D_MODEL), D_FF ** -0.5),
    }


def reference(x_prompt, x_sample, state_conv_b, state_ffn, cache_k, cache_v, cache_logf,
              norm_mix, norm_ffn, w_in_ab, sgu_norm, w_spatial, b_spatial, conv_b, w_out_ab,
              w_in_c, b_forget, q_norm, k_norm, w_out_c, w_up, conv_ffn, w_down):
    xp, xs = x_prompt, x_sample
    conv_b_p, conv_b_s, gmlp_v_s = [], [], []
    k_p, v_p, lf_p, k_s, v_s, lf_s = [], [], [], [], [], []
    ffn_p, ffn_s = [], []
    for layer in range(DEPTH):
        i = layer // 2
        hp = _rmsnorm(xp, norm_mix[layer])
        hs = _rmsnorm(xs, norm_mix[layer])
        if layer % 2 == 0:
            zero_b = jnp.zeros((xp.shape[0], CONV_W - 1, D_B), xp.dtype)
            yp, cst_p, _ = _mixer_ab(hp, zero_b, w_in_ab[i], sgu_norm[i], w_spatial[i], b_spatial[i], conv_b[i], w_out_ab[i])
            ys, cst_s, v_new = _mixer_ab(hs, state_conv_b[i], w_in_ab[i], sgu_norm[i], w_spatial[i], b_spatial[i], conv_b[i], w_out_ab[i])
            conv_b_p.append(cst_p)
            conv_b_s.append(cst_s)
            gmlp_v_s.append(v_new)
        else:
            yp, kp_, vp_, lfp = _mixer_c_prompt(hp, w_in_c[i], b_forget[i], q_norm[i], k_norm[i], w_out_c[i])
            ys, ks_, vs_, lfs = _mixer_c_sample(hs, cache_k[i], cache_v[i], cache_logf[i], w_in_c[i], b_forget[i], q_norm[i], k_norm[i], w_out_c[i])
            k_p.append(kp_)
            v_p.append(vp_)
            lf_p.append(lfp.astype(x_prompt.dtype))
            k_s.append(ks_)
            v_s.append(vs_)
            lf_s.append(lfs.astype(x_sample.dtype))
        xp = xp + yp
        xs = xs + ys
        hp = _rmsnorm(xp, norm_ffn[layer])
        hs = _rmsnorm(xs, norm_ffn[layer])
        zero_f = jnp.zeros((xp.shape[0], CONV_W - 1, 2 * D_FF), xp.dtype)
        yp, st_p = _conv_ffn(hp, zero_f, w_up[layer], conv_ffn[layer], w_down[layer])
        ys, st_s = _conv_ffn(hs, state_ffn[layer], w_up[layer], conv_ffn[layer], w_down[layer])
        ffn_p.append(st_p)
        ffn_s.append(st_s)
        xp = xp + yp
        xs = xs + ys
    return (xp, xs,
            jnp.stack(conv_b_p), jnp.stack(conv_b_s), jnp.stack(gmlp_v_s),
            jnp.stack(k_p), jnp.stack(v_p), jnp.stack(lf_p),
            jnp.stack(k_s), jnp.stack(v_s), jnp.stack(lf_s),
            jnp.stack(ffn_p), jnp.stack(ffn_s))
```

```python
import os
import sys
from contextlib import ExitStack

import numpy as np
import concourse.bass as bass
import concourse.mybir as mybir
from concourse.bass_utils import run_bass_kernel_spmd

F32 = mybir.dt.float32
BF16 = mybir.dt.bfloat16
AF = mybir.ActivationFunctionType
ALU = mybir.AluOpType

D = 1024
DUP = 5632
DFF = 2816
NH = 16
TN = 512
SL = 16
EPS = 1e-6
NEGM = -30000.0
GELU = AF.Gelu_apprx_tanh
ENGS = ("pe", "act", "dve", "pool", "sp")
EPOCH = 20000


class Op:
    __slots__ = ("fn", "waits", "dma", "sig", "cnt", "tag")

    def __init__(self, fn, waits, dma):
        self.fn, self.waits, self.dma, self.sig, self.cnt = fn, waits, dma, False, 0
        f = sys._getframe(2)
        tg = []
        while f is not None and len(tg) < 4:
            tg.append(str(f.f_lineno))
            f = f.f_back
        self.tag = "<".join(tg)


class Prog:
    def __init__(self):
        self.ops = {e: [] for e in ENGS}
        self.lastw, self.readers, self.dcount = {}, {}, {}

    def _need(self, eng, dma, ref, raw):
        if ref[0] == "d" or ref[1] != eng or dma is not None:
            return True
        if eng == "pe":
            return False
        return True

    def add(self, eng, fn, r=(), w=(), dma=None, extra=()):
        idx = len(self.ops[eng])
        if dma is not None:
            c = self.dcount.get(dma, 0) + 1
            self.dcount[dma] = c
            me = ("d", dma, c)
        else:
            me = ("e", eng, idx)
        waits = set(extra)
        for b in r:
            lw = self.lastw.get(b)
            if lw is not None and self._need(eng, dma, lw, True):
                waits.add(lw)
        for b in w:
            lw = self.lastw.get(b)
            if lw is not None and self._need(eng, dma, lw, False):
                waits.add(lw)
            for rr in self.readers.get(b, {}).values():
                if rr != me and self._need(eng, dma, rr, False):
                    waits.add(rr)
        for b in w:
            self.lastw[b] = me
            self.readers[b] = {}
        for b in r:
            self.readers.setdefault(b, {})[me[:2]] = me
        self.ops[eng].append(Op(fn, waits, dma))
        return me

    def emit(self, nc, es):
        for e in ENGS:
            for op in self.ops[e]:
                for ref in op.waits:
                    if ref[0] == "e":
                        self.ops[ref[1]][ref[2]].sig = True
        esem = {}
        for e in ENGS:
            c = 0
            for op in self.ops[e]:
                if op.sig:
                    c += 1
                    op.cnt = c
            esem[e] = [es.enter_context(nc.semaphore(f"s_{e}_{k}")) for k in range(c // EPOCH + 1)]
        dsem = {k: es.enter_context(nc.semaphore(f"d_{k}")) for k in self.dcount}
        block = es.enter_context(nc.Block())
        engobj = {"pe": block.tensor, "act": block.scalar, "dve": block.vector, "pool": block.gpsimd, "sp": block.sync}

        def body_for(ename):
            def body(e):
                waited = {}
                for op in self.ops[ename]:
                    for ref in sorted(op.waits, key=str):
                        if ref[0] == "e":
                            c = self.ops[ref[1]][ref[2]].cnt
                            k = ("e", ref[1])
                            if waited.get(k, 0) >= c:
                                continue
                            waited[k] = c
                            ep = (c - 1) // EPOCH
                            e.wait_ge(esem[ref[1]][ep], c - ep * EPOCH)
                        else:
                            k = ("d", ref[1])
                            if waited.get(k, 0) >= ref[2]:
                                continue
                            waited[k] = ref[2]
                            e.wait_ge(dsem[ref[1]], 16 * ref[2])
                    try:
                        ins = op.fn(e)
                    except Exception as ex:
                        raise RuntimeError(f"emit failed for op at line {op.tag} on {ename}: {str(ex)[:300]}") from None
                    if ins is None:
                        continue
                    if op.dma is not None:
                        ins.then_inc(dsem[op.dma], 16)
                    elif op.sig:
                        ep = (op.cnt - 1) // EPOCH
                        ins.then_inc(esem[ename][ep], 1)
            return body

        for ename in ENGS:
            engobj[ename](body_for(ename))


class _Stop(Exception):
    pass


def build(T, NS, PAST, stop=None):
    NT = T // TN
    NBLK = T // 128
    NSTOK = NS * SL
    NPB = PAST // 128
    VBLK = max(NBLK, 18)
    KW = max(T, PAST + SL)
    nc = bass.Bass("TRN2", target_bir_lowering=False)
    P = Prog()
    es = ExitStack()

    def din(name, shape):
        return nc.dram_tensor(name, list(shape), F32, kind="ExternalInput").ap()

    def dout(name, shape):
        return nc.dram_tensor(name, list(shape), F32, kind="ExternalOutput").ap()

    def dscr(name, shape):
        return nc.dram_tensor(name, list(shape), BF16, kind="Internal").ap()

    xp = din("xp", [T, D]); xs = din("xs", [NSTOK, D])
    st_cb = din("st_cb", [NS * 2, 512]); st_ffn = din("st_ffn", [2, NS * 2, DUP])
    ck = din("ck", [NS, PAST, D]); cv = din("cv", [NS, PAST, D]); clf = din("clf", [NS, PAST, NH])
    norm_mix = din("norm_mix", [2, D]); norm_ffn = din("norm_ffn", [2, D])
    w_in_ab = din("w_in_ab", [D, 2560]); sgu_norm = din("sgu_norm", [4, 128])
    w_spatial = din("w_spatial", [4, 128, 128]); b_spatial = din("b_spatial", [4, 128])
    conv_b = din("conv_b", [3, 512]); w_out_ab = din("w_out_ab", [D, D])
    w_in_c = din("w_in_c", [D, 3088]); b_forget = din("b_forget", [NH, 1])
    q_norm = din("q_norm", [64, 1]); k_norm = din("k_norm", [64, 1]); w_out_c = din("w_out_c", [D, D])
    w_up = din("w_up", [2, D, DUP]); conv_ffn = din("conv_ffn", [2, 3, DUP]); w_down = din("w_down", [2, DFF, D])
    cst = din("cst", [128, 384])

    o_yp = dout("o_yp", [T, D]); o_ys = dout("o_ys", [NSTOK, D])
    o_cbp = dout("o_cbp", [2, 512]); o_cbs = dout("o_cbs", [NS * 2, 512]); o_gv = dout("o_gv", [NSTOK, 512])
    o_kp = dout("o_kp", [T, D]); o_vp = dout("o_vp", [T, D]); o_lfp = dout("o_lfp", [T, NH])
    o_ks = dout("o_ks", [NSTOK, D]); o_vs = dout("o_vs", [NSTOK, D]); o_lfs = dout("o_lfs", [NSTOK, NH])
    o_ffp = dout("o_ffp", [2, 2, DUP]); o_ffs = dout("o_ffs", [2, NS * 2, DUP])
    kscr = dscr("kscr", [NH, 64, T])

    def sb(name, shape, dt=F32):
        return es.enter_context(nc.sbuf_tensor(name, list(shape), dt))

    cstf = sb("cstf", [128, 384])
    identf = cstf[:, 0:128]; negmask = cstf[:, 128:256]; negmaskb = cstf[:, 256:320].bitcast(BF16)
    identb = sb("identb", [128, 128], BF16); onesb = sb("onesb", [128, 128], BF16); blkones = sb("blkones", [128, 128], BF16)
    col = sb("col", [128, 320]); qkcol = sb("qkcol", [128, 2]); negb = sb("negb", [NH, 1])
    epsc = sb("epsc", [128, 1]); onec = sb("onec", [128, 1]); onesf = sb("onesf", [NH, TN])
    stg = sb("stg", [128, 2, 1024])
    xT = sb("xT", [128, 8, TN]); H = sb("H", [128, 8, TN], BF16); G = sb("G", [128, 22, TN], BF16)
    V = sb("V", [128, VBLK, 1024], BF16)
    kbuf = [sb("kbufE", [128, KW], BF16), sb("kbufO", [128, KW], BF16)]
    wring = sb("wring", [128, 3, 4096], BF16)
    zbuf = sb("zbuf", [128, 2, 2 + TN]); tmp = sb("tmp", [128, 4, TN]); rs = sb("rs", [128, 2, TN])
    pbhP = sb("pbhP", [128, 4, 1, 2]); pbhS = sb("pbhS", [128, 4, NS, 2])
    zhP = sb("zhP", [128, 2, 44, 1, 2]); zhS = sb("zhS", [128, 2, 44, NS, 2])
    E = sb("E", [128, 3, TN], BF16)
    gv = sb("gv", [128, 1, 512]); vnf = sb("vnf", [128, 512]); vtok = sb("vtok", [128, 1, 512], BF16)
    vnb = sb("vnb", [128, 512]); vss = sb("vss", [128, 8])
    WmT = sb("WmT", [128, 4, 128], BF16); bsb = sb("bsb", [1, 4, 128], BF16)
    ndtok = sb("ndtok", [128, NBLK, NH]); ndtoks = sb("ndtoks", [SL, NS, NH])
    nl = sb("nl", [NH, TN]); nd = sb("nd", [NH, TN]); ndcarry = sb("ndcarry", [NH, 1]); r1 = rs[0:NH, 0, :]
    spl = sb("spl", [NH, 3, TN], BF16)
    lfst = sb("lfst", [128, 4, NH]); kost = sb("kost", [128, 1, 4, 128]); kst = None
    sqh = E; kTs = sb("kTs", [128, 8, 64], BF16)

    ps = [es.enter_context(nc.psum_tensor(f"ps{i}", [128, 512], F32)) for i in range(8)]
    Gq = G
    sq = G[:, 14:22, :]

    def vview(b0, nb, dt, pat=None, **kw):
        a = V[:, b0:b0 + nb, :].rearrange("p b c -> p (b c)")
        if dt == F32:
            a = a.bitcast(F32)
        return a if pat is None else a.rearrange(pat, **kw)

    kc = [vview(0, 2, BF16, "p (k c) -> p k c", c=128), vview(2, 2, BF16, "p (k c) -> p k c", c=128)]
    vc = [vview(4, 2, BF16, "p (k c) -> p k c", c=128), vview(6, 2, BF16, "p (k c) -> p k c", c=128)]
    nlc = vview(8, 4, F32)
    biasc = vview(12, 1, F32, "p (k h) -> p k h", h=NH)
    ltok = vview(13, 1, F32, "p (k h) -> p k h", h=NH)
    Vnew = vview(14, 4, BF16, "p (s c) -> p s c", c=1024)
    VK = lambda b0, nb: [("V", b) for b in range(b0, b0 + nb)]

    def mm(out, lhsT, rhs, start=True, stop=True, r=(), w=()):
        return P.add("pe", lambda e: e.matmul(out, lhsT=lhsT, rhs=rhs, start=start, stop=stop), r, w)

    def tr(out, in_, ident, r=(), w=()):
        return P.add("pe", lambda e: e.transpose(out=out, in_=in_, identity=ident), r, w)

    def act(out, in_, func, r=(), w=(), **kw):
        return P.add("act", lambda e: e.activation(out=out, in_=in_, func=func, **kw), r, w)

    def tt(eng, out, in0, in1, op, r=(), w=()):
        return P.add(eng, lambda e: e.tensor_tensor(out=out, in0=in0, in1=in1, op=op), r, w)

    def ts(eng, out, in0, s1, s2, op0, op1=ALU.bypass, r=(), w=()):
        return P.add(eng, lambda e: e.tensor_scalar(out=out, in0=in0, scalar1=s1, scalar2=s2, op0=op0, op1=op1), r, w)

    def stt(out, in0, scalar, in1, op0, op1, r=(), w=()):
        return P.add("dve", lambda e: e.scalar_tensor_tensor(out=out, in0=in0, scalar=scalar, in1=in1, op0=op0, op1=op1), r, w)

    def cp(eng, out, in_, r=(), w=()):
        return P.add(eng, lambda e: e.tensor_copy(out=out, in_=in_), r, w)

    def recip(out, in_, r=(), w=()):
        return P.add("dve", lambda e: e.reciprocal(out=out, in_=in_), r, w)

    def mset(eng, ap, val, w=()):
        return P.add(eng, lambda e: e.memset(ap, val), (), w)

    def dma(q, out, in_, key, r=(), w=(), **kw):
        return P.add(q, lambda e: e.dma_start(out=out, in_=in_, **kw), r, w, dma=key)

    ctr = {"mm": 0, "tb": 0, "stg": 0, "wr": 0, "zb": 0, "tmp": 0, "E": 0, "sb": 0, "ko": 0, "sb3": 0}

    def rot(name, n, base=0):
        v = ctr[name]
        ctr[name] = (v + 1) % n
        return base + v

    mmbank = lambda: rot("mm", 4)
    tbank = lambda: rot("tb", 2, 6)
    final_refs = []

    def v3(ap, nseq, L):
        return ap if nseq == 1 and False else ap.rearrange("p (s l) -> p s l", s=nseq)

    groups = {}

    def defgroup(name, src, KC, slots):
        wmax = max(sum(c1 - c0 for c0, c1 in s) for s in slots)
        scr = dscr("w_" + name, [len(slots), 128, KC * wmax])
        groups[name] = (scr, KC, slots, wmax)
        srcv = src.rearrange("(k p) n -> p k n", p=128)
        last = None
        for si, segs in enumerate(slots):
            wtot = sum(c1 - c0 for c0, c1 in segs)
            o = 0
            for c0, c1 in segs:
                dst = scr[si, :, 0:KC * wtot].rearrange("p (k w) -> p k w", k=KC)[:, :, o:o + c1 - c0]
                last = dma("pool", dst, srcv[:, :, c0:c1], "cast_" + name, w=[("scr", name, si, o)])
                o += c1 - c0
        for si, segs in enumerate(slots):
            o = 0
            for c0, c1 in segs:
                P.lastw[("scr", name, si, o)] = last
                o += c1 - c0

    def wload(name, si):
        scr, KC, slots, wmax = groups[name]
        segs = slots[si]
        wtot = sum(c1 - c0 for c0, c1 in segs)
        s = rot("wr", 3)
        keys, o = [], 0
        for c0, c1 in segs:
            keys.append(("scr", name, si, o)); o += c1 - c0
        dma("sp", wring[:, s, 0:KC * wtot], scr[si, :, 0:KC * wtot], f"wr{s}", r=keys, w=[("wr", s)])
        return s, wtot

    def wr(s, W, k, c0, m):
        return wring[:, s, k * W + c0:k * W + c0 + m]

    rowsrc = [(norm_mix.rearrange("l (c p) -> (l c) p", p=128), 16), (norm_ffn.rearrange("l (c p) -> (l c) p", p=128), 16),
              (conv_b.rearrange("r (c p) -> (r c) p", p=128), 12), (conv_ffn.rearrange("l r (c p) -> (l r c) p", p=128), 264)]
    CM = lambda l, c: l * 8 + c
    CF = lambda l, c: 16 + l * 8 + c
    CB = lambda r_, c: 32 + r_ * 4 + c
    CW = lambda l, r_, ch: 44 + (l * 3 + r_) * 44 + ch
    rowt = tmp[:, 1:4, 0:128]
    ikeys = []

    def idma(out, in_, key):
        ikeys.append(key)
        dma("sp", out, in_, "init", w=[key])

    idma(cstf[:, :], cst[:, :], "cstf")
    row0 = 0
    for src, n in rowsrc:
        done = 0
        while done < n:
            t_i, off = divmod(row0 + done, 128)
            m = min(n - done, 128 - off)
            idma(rowt[off:off + m, t_i, :], src[done:done + m, :], ("rowt", t_i, off))
            done += m
        row0 += n
    for j, srcn in enumerate((q_norm, k_norm)):
        for h2 in range(2):
            idma(qkcol[64 * h2:64 * h2 + 64, j:j + 1], srcn[:, :], ("qkcol", j, h2))
    idma(negb[:, :], b_forget[:, :], "negb")
    idma(stg[0:1, 0, 0:512], sgu_norm.rearrange("g c -> (g c)").rearrange("(o n) -> o n", o=1), ("stg", 0))
    idma(stg[0:1, 0, 512:1024], b_spatial.rearrange("g c -> (g c)").rearrange("(o n) -> o n", o=1), ("stgi", 0))
    for g in range(4):
        idma(stg[:, 1, 128 * g:128 * g + 128], w_spatial[g, :, :], ("stgi", 1, g))
    idma(stg[0:2 * NS, 1, 512:1024], st_cb[:, :], ("stg", 1))
    for k_ in ikeys:
        P.lastw[k_] = ("d", "init", P.dcount["init"])
    INIT = [("stg", 0), ("stg", 1)]

    mset("pool", G[:, :, :], 0.0, w=[("G", c) for c in range(22)])
    mset("pool", kbuf[0][64:128, :], 0.0, w=[("kb", 0)])
    mset("pool", kbuf[0][64:67, :], 1.0, w=[("kb", 0)])
    mset("pool", kbuf[1][0:64, :], 0.0, w=[("kb", 1)])
    mset("pool", kbuf[1][0:3, :], 1.0, w=[("kb", 1)])
    mset("pool", onesb[:, :], 1.0, w=["onesb"])
    mset("pool", blkones[:, :], 0.0, w=["blkones"])
    mset("pool", blkones[0:64, 0:64], 1.0, w=["blkones"])
    mset("pool", blkones[64:128, 64:128], 1.0, w=["blkones"])
    mset("pool", epsc[:, :], EPS, w=["epsc"])
    mset("pool", onec[:, :], 1.0, w=["onec"])
    mset("pool", onesf[:, :], 1.0, w=["onesf"])
    mset("pool", ndcarry[:, :], 0.0, w=["ndcarry"])
    mset("pool", pbhP[:, :, :, :], 0.0, w=["pbhP"])
    mset("pool", zhP[:, :, :, :, :], 0.0, w=[("zhP", l) for l in range(2)])
    mset("pool", tmp[0:1, 0, 0:128], 1.0, w=[("tmp", 0)])
    cp("dve", identb[:, :], identf, r=["cstf"], w=["identb"])
    cp("dve", negmaskb, negmask, r=["cstf"], w=["negmaskb"])

    defgroup("inab", w_in_ab, 8, [[(0, 512)], [(512, 1024)]] +
             [[(1536 + 128 * c, 1664 + 128 * c), (2048 + 128 * c, 2176 + 128 * c), (1024 + 128 * c, 1152 + 128 * c)] for c in range(4)])
    defgroup("outab", w_out_ab, 8, [[(0, 512)], [(512, 1024)]])
    defgroup("up0", w_up[0], 8, [[(256 * j, 256 * j + 256), (DFF + 256 * j, DFF + 256 * j + 256)] for j in range(11)])
    defgroup("down0", w_down[0], 22, [[(128 * o, 128 * o + 128)] for o in range(8)])
    defgroup("inc", w_in_c, 8, [[(512 * j, 512 * j + 512)] for j in range(6)] + [[(3072, 3088)]])
    defgroup("outc", w_out_c, 8, [[(0, 512)], [(512, 1024)]])
    defgroup("up1", w_up[1], 8, [[(256 * j, 256 * j + 256), (DFF + 256 * j, DFF + 256 * j + 256)] for j in range(11)])
    defgroup("down1", w_down[1], 22, [[(128 * o, 128 * o + 128)] for o in range(8)])

    for t_i in range(3):
        nr = min(128, row0 - 128 * t_i)
        rk = [k_ for k_ in ikeys if isinstance(k_, tuple) and k_[0] == "rowt" and k_[1] == t_i]
        tr(ps[0][:, 0:nr], rowt[0:nr, t_i, :], identf[0:nr, 0:nr], r=rk + ["cstf", ("tmp", t_i + 1)], w=[("ps", 0)])
        cp("dve", col[:, 128 * t_i:128 * t_i + nr], ps[0][:, 0:nr], r=[("ps", 0)], w=["col"])
    ts("dve", qkcol[:, 0:1], qkcol[:, 0:1], 0.125, None, ALU.mult, r=[("qkcol", 0, 0), ("qkcol", 0, 1), ("qkcol", 1, 0), ("qkcol", 1, 1)], w=["qkcol"])
    ts("dve", negb[:, :], negb[:, :], -1.0, None, ALU.mult, r=["negb"], w=["negb"])
    mm(ps[1][:, 0:512], tmp[0:1, 0, 0:128], stg[0:1, 0, 0:512], r=[("stg", 0), ("tmp", 0)], w=[("ps", 1)])
    cp("dve", vnb[:, :], ps[1][:, 0:512], r=[("ps", 1)], w=["vnb"])
    cp("dve", bsb[0:1, :, :], stg[0:1, 0, 512:1024].rearrange("p (g c) -> p g c", g=4), r=[("stgi", 0), ("stg", 0)], w=["bsb"])
    for g in range(4):
        tr(ps[2][:, 128 * g:128 * g + 128], stg[:, 1, 128 * g:128 * g + 128], identf, r=[("stgi", 1, g), ("stg", 1), "cstf"], w=[("ps", 2)])
    cp("dve", WmT[:, :, :], ps[2][:, :].rearrange("p (g c) -> p g c", g=4), r=[("ps", 2)], w=["WmT"])
    mset("dve", WmT[64:128, :, 0:64], 0.0, w=["WmT"])
    for cc in range(4):
        tr(ps[3][:, cc * 2 * NS:(cc + 1) * 2 * NS], stg[0:2 * NS, 1, 512 + 128 * cc:512 + 128 * cc + 128], identf[0:2 * NS, 0:2 * NS],
           r=[("stg", 1), "cstf"], w=[("ps", 3)])
    cp("dve", pbhS[:, :, :, :].rearrange("p c s r -> p c (s r)"), ps[3][:, 0:8 * NS].rearrange("p (c x) -> p c x", c=4),
       r=[("ps", 3)], w=["pbhS"])
    for l in range(2):
        for g in range(11):
            st = rot("stg", 2)
            b = mmbank()
            dma("sp", stg[0:2 * NS, st, 0:512], st_ffn[l, :, 512 * g:512 * g + 512], f"stg{st}", w=[("stg", st)])
            for cc in range(4):
                tr(ps[b][:, cc * 2 * NS:(cc + 1) * 2 * NS], stg[0:2 * NS, st, 128 * cc:128 * cc + 128], identf[0:2 * NS, 0:2 * NS],
                   r=[("stg", st), "cstf"], w=[("ps", b)])
            cp("dve", zhS[:, l, 4 * g:4 * g + 4, :, :].rearrange("p c s r -> p c (s r)"),
               ps[b][:, 0:8 * NS].rearrange("p (c x) -> p c x", c=4), r=[("ps", b)], w=[("zhS", l)])

    def load_x(src, N):
        for b in range((N + 127) // 128):
            nb = min(128, N - b * 128)
            st = rot("stg", 2)
            dma("sp", stg[0:nb, st, :], src[b * 128:b * 128 + nb, :], f"stg{st}", w=[("stg", st)])
            for g in range(2):
                bank = tbank()
                for cc in range(4):
                    c = 4 * g + cc
                    tr(ps[bank][:, cc * 128:cc * 128 + nb], stg[0:nb, st, c * 128:(c + 1) * 128], identf[0:nb, 0:nb],
                       r=[("stg", st), "cstf"], w=[("ps", bank)])
                act(xT[:, 4 * g:4 * g + 4, b * 128:b * 128 + nb], ps[bank][:, :].rearrange("p (c t) -> p c t", c=4)[:, :, 0:nb], AF.Copy,
                    r=[("ps", bank)], w=[("xT", 4 * g + cc) for cc in range(4)])

    def norm(gc0, N):
        for c in range(8):
            act(sq[:, c, 0:N], xT[:, c, 0:N], AF.Square, r=[("xT", c)], w=[("G", 14 + c)])
            mm(ps[5][:, 0:N], onesb[:, :], sq[:, c, 0:N], start=(c == 0), stop=(c == 7), r=[("G", 14 + c), "onesb"], w=[("ps", 5)])
        act(rs[:, 0, 0:N], ps[5][:, 0:N], AF.Ln, scale=1.0 / D, bias=epsc[:, 0:1], r=[("ps", 5), "epsc"], w=[("rs", 0)])
        act(rs[:, 0, 0:N], rs[:, 0, 0:N], AF.Exp, scale=-0.5, r=[("rs", 0)], w=[("rs", 0)])
        for c in range(8):
            stt(H[:, c, 0:N], xT[:, c, 0:N], col[:, gc0 + c:gc0 + c + 1], rs[:, 0, 0:N], ALU.mult, ALU.mult,
                r=[("xT", c), ("rs", 0), "col"], w=[("H", c)])

    def proj_fm(s, W, c0, N, rhs_of, rkeys, bank):
        for k in range(8):
            mm(ps[bank][:, 0:N], wr(s, W, k, c0, 128), rhs_of(k), start=(k == 0), stop=(k == 7),
               r=[("wr", s), rkeys(k)], w=[("ps", bank)])

    def resid(oc, bank, N):
        tt("dve", xT[:, oc, 0:N], xT[:, oc, 0:N], ps[bank][:, 0:N], ALU.add, r=[("xT", oc), ("ps", bank)], w=[("xT", oc)])

    def conv3(zb, zkey, cur, curkeys, A, akey, l_cols, nseq, L):
        w0, w1, w2 = l_cols
        act(A, cur, AF.Copy, scale=col[:, w2:w2 + 1], r=list(curkeys) + ["col"], w=[akey])
        stt(A, zb[:, :, 1:1 + L], col[:, w1:w1 + 1], A, ALU.mult, ALU.add, r=[zkey, akey, "col"], w=[akey])
        stt(A, zb[:, :, 0:L], col[:, w0:w0 + 1], A, ALU.mult, ALU.add, r=[zkey, akey, "col"], w=[akey])

    def mixer_ab(sample, N, nseq, L):
        pbh = pbhS if sample else pbhP
        pbk = "pbhS" if sample else "pbhP"
        norm(CM(0, 0), N)
        uT = G[:, 0:4, :]
        ycat = G[:, 4:12, :]
        s, W = wload("inab", 0)
        for oc in range(4):
            b = mmbank()
            proj_fm(s, W, oc * 128, N, lambda k: H[:, k, 0:N], lambda k: ("H", k), b)
            act(uT[:, oc, 0:N], ps[b][:, 0:N], GELU, r=[("ps", b)], w=[("G", oc)])
        s, W = wload("inab", 1)
        nb = SL if sample else 128

        def gview(c0, nch, dt):
            a = G[:, c0:c0 + nch, :].rearrange("p c t -> p (c t)")
            return a.bitcast(F32) if dt == F32 else a

        gvb = [(gv[:, 0, :], [("gv", 0)]), (gview(12, 2, F32), [("G", 12), ("G", 13)])]
        vnfb = [(vnf, ["vnf"]), (gview(14, 2, F32), [("G", 14), ("G", 15)])]
        vtokb = [(vtok[:, 0, :], [("vtok", 0)]), (gview(16, 1, BF16), [("G", 16)])]
        nj = N // nb
        pbank = {}

        def vproj(j):
            b = mmbank()
            pbank[j] = b
            for k in range(8):
                mm(ps[b][0:nb, 0:512], H[:, k, j * nb:(j + 1) * nb], wr(s, W, k, 0, 512), start=(k == 0), stop=(k == 7),
                   r=[("wr", s), ("H", k)], w=[("ps", b)])

        def vchain(j):
            b = pbank[j]
            gvt, gk = gvb[j % 2]
            vnt, vk_ = vnfb[j % 2]
            vtt, tk = vtokb[j % 2]
            vs = vss[:, 0:8] if j % 2 == 0 else None
            act(gvt[0:nb, :], ps[b][0:nb, :], GELU, r=[("ps", b)], w=gk)
            for g in range(4):
                act(vnt[0:nb, 128 * g:128 * g + 128], gvt[0:nb, 128 * g:128 * g + 128], AF.Square, accum_out=vss[0:nb, g:g + 1],
                    r=gk, w=vk_ + ["vss"])
            act(vss[0:nb, 4:8], vss[0:nb, 0:4], AF.Sqrt, scale=1.0 / 128, bias=epsc[0:nb, 0:1], r=["vss", "epsc"], w=["vss2"])
            recip(vss[0:nb, 4:8], vss[0:nb, 4:8], r=["vss2"], w=["vss2"])
            for g in range(4):
                stt(vnt[0:nb, 128 * g:128 * g + 128], gvt[0:nb, 128 * g:128 * g + 128], vss[0:nb, 4 + g:5 + g], vnb[0:nb, 128 * g:128 * g + 128],
                    ALU.mult, ALU.mult, r=gk + ["vss2", "vnb"], w=vk_)
            cp("dve", vtt[0:nb, :], vnt[0:nb, :], r=vk_, w=tk)
            if sample:
                final_refs.append(dma("pool", o_gv[j * SL:(j + 1) * SL, :], vnt[0:nb, :], f"o_gv{j % 2}", r=vk_, w=[("o_gv", j)]))

        def vspatial(j):
            vtt, tk = vtokb[j % 2]
            for g in range(4):
                mm(ps[4 + g][:, j * nb:(j + 1) * nb], onesb[0:1, :], bsb[0:1, g, 0:nb], start=True, stop=False,
                   r=["onesb", "bsb"], w=[("ps", 4 + g)])
                mm(ps[4 + g][:, j * nb:(j + 1) * nb], vtt[0:nb, 128 * g:128 * g + 128], WmT[0:nb, g, 0:nb], start=False, stop=True,
                   r=tk + ["WmT"], w=[("ps", 4 + g)])

        vproj(0)
        for j in range(nj):
            vchain(j)
            if j + 1 < nj:
                vproj(j + 1)
            vspatial(j)
        for g in range(4):
            tt("dve", ycat[:, g, 0:N], uT[:, g, 0:N], ps[4 + g][:, 0:N], ALU.mult, r=[("G", g), ("ps", 4 + g)], w=[("G", 4 + g)])
        for c in range(4):
            s, W = wload("inab", 2 + c)
            bgc, bxb, bgb = mmbank(), mmbank(), mmbank()
            for bank, off in ((bgc, 0), (bxb, 128), (bgb, 256)):
                proj_fm(s, W, off, N, lambda k: H[:, k, 0:N], lambda k: ("H", k), bank)
            t0 = rot("tmp", 4)
            act(tmp[:, t0, 0:N], ps[bxb][:, 0:N], AF.Copy, r=[("ps", bxb)], w=[("tmp", t0)])
            z = rot("zb", 2)
            zb = zbuf[:, z, 0:nseq * (2 + L)].rearrange("p (s l) -> p s l", s=nseq)
            cp("dve", zb[:, :, 0:2], pbh[:, c, :, :], r=[pbk], w=[("zb", z)])
            tt("dve", zb[:, :, 2:2 + L], v3(ps[bgc][:, 0:N], nseq, L), v3(tmp[:, t0, 0:N], nseq, L), ALU.mult,
               r=[("ps", bgc), ("tmp", t0)], w=[("zb", z)])
            t1 = rot("tmp", 4)
            conv3(zb, ("zb", z), zb[:, :, 2:2 + L], [("zb", z)], v3(tmp[:, t1, 0:N], nseq, L), ("tmp", t1), (CB(0, c), CB(1, c), CB(2, c)), nseq, L)
            cp("dve", pbh[:, c, :, :], zb[:, :, L:L + 2], r=[("zb", z)], w=[pbk])
            tt("dve", ycat[:, 4 + c, 0:N], ps[bgb][:, 0:N], tmp[:, t1, 0:N], ALU.mult, r=[("ps", bgb), ("tmp", t1)], w=[("G", 8 + c)])
        for half in range(2):
            s, W = wload("outab", half)
            for o in range(4):
                b = mmbank()
                proj_fm(s, W, o * 128, N, lambda k: ycat[:, k, 0:N], lambda k: ("G", 4 + k), b)
                resid(half * 4 + o, b, N)

    def ffn(l, sample, N, nseq, L):
        zh = zhS if sample else zhP
        zk = ("zhS", l) if sample else ("zhP", l)
        norm(CF(l, 0), N)
        pending = None

        def product(jq, res):
            tt("dve", G[:, jq, 0:N], tmp[:, res["g"], 0:N], tmp[:, res["u"], 0:N], ALU.mult,
               r=[("tmp", res["g"]), ("tmp", res["u"])], w=[("G", jq)])

        for j in range(11):
            s, W = wload(f"up{l}", j)
            for q in range(2):
                res = {}
                chunks = (("g", q * 128, 2 * j + q), ("u", 256 + q * 128, 22 + 2 * j + q))
                zs = [rot("zb", 2), rot("zb", 2)]
                zbs = [zbuf[:, z, 0:nseq * (2 + L)].rearrange("p (s l) -> p s l", s=nseq) for z in zs]
                for (which, off, ch), z, zb in zip(chunks, zs, zbs):
                    cp("pool", zb[:, :, 0:2], zh[:, l, ch, :, :], r=[zk], w=[("zb", z)])
                for (which, off, ch), z, zb in zip(chunks, zs, zbs):
                    b = mmbank()
                    proj_fm(s, W, off, N, lambda k: H[:, k, 0:N], lambda k: ("H", k), b)
                    act(zb[:, :, 2:2 + L], v3(ps[b][:, 0:N], nseq, L), AF.Copy, r=[("ps", b)], w=[("zb", z)])
                    t0 = rot("tmp", 4)
                    conv3(zb, ("zb", z), v3(ps[b][:, 0:N], nseq, L), [("ps", b)], v3(tmp[:, t0, 0:N], nseq, L), ("tmp", t0),
                          (CW(l, 0, ch), CW(l, 1, ch), CW(l, 2, ch)), nseq, L)
                    cp("pool", zh[:, l, ch, :, :], zb[:, :, L:L + 2], r=[("zb", z)], w=[zk])
                    res[which] = t0
                act(tmp[:, res["g"], 0:N], tmp[:, res["g"], 0:N], AF.Silu, r=[("tmp", res["g"])], w=[("tmp", res["g"])])
                product(2 * j + q, res)
        for oc in range(8):
            s, W = wload(f"down{l}", oc)
            b = mmbank()
            for k in range(22):
                mm(ps[b][:, 0:N], wr(s, W, k, 0, 128), G[:, k, 0:N], start=(k == 0), stop=(k == 21), r=[("wr", s), ("G", k)], w=[("ps", b)])
            resid(oc, b, N)

    def attn(sample, ti, N):
        tok0 = 0 if sample else ti * TN
        o_k, o_v, o_lf = (o_ks, o_vs, o_lfs) if sample else (o_kp, o_vp, o_lfp)
        norm(CM(1, 0), N)
        oT = H
        tasks = [(which, slots[half], half * 4 + o, o) for which, slots in (("q", (0, 1)), ("k", (2, 3))) for half in range(2) for o in range(4)]
        st8 = {}

        def stageA(i):
            which, slot, p, o = tasks[i]
            if o == 0:
                st8["sw"] = wload("inc", slot)
            s, W = st8["sw"]
            b = mmbank()
            proj_fm(s, W, o * 128, N, lambda k: H[:, k, 0:N], lambda k: ("H", k), b)
            si = i % 2
            act(sqh[:, si, 0:N], ps[b][:, 0:N], AF.Square, r=[("ps", b)], w=[("E", si)])
            st8[i] = (b, si)

        def stageB(i):
            which, slot, p, o = tasks[i]
            b, si = st8[i]
            b2 = 4 + i % 2
            mm(ps[b2][:, 0:N], blkones[:, :], sqh[:, si, 0:N], r=[("E", si), "blkones"], w=[("ps", b2)])
            act(rs[:, 1, 0:N], ps[b2][:, 0:N], AF.Ln, scale=1.0 / 64, bias=epsc[:, 0:1], r=[("ps", b2), "epsc"], w=[("rs", 1)])
            act(rs[:, 1, 0:N], rs[:, 1, 0:N], AF.Exp, scale=-0.5, r=[("rs", 1)], w=[("rs", 1)])
            if which == "q":
                for hh in range(2):
                    rw = slice(64 * hh, 64 * hh + 64)
                    stt(Gq[rw, 2 * p + hh, 0:N], ps[b][rw, 0:N], qkcol[rw, 0:1], rs[rw, 1, 0:N], ALU.mult, ALU.mult,
                        r=[("ps", b), ("rs", 1), "qkcol"], w=[("G", 2 * p + hh)])
            else:
                t0 = rot("tmp", 4)
                stt(tmp[:, t0, 0:N], ps[b][:, 0:N], qkcol[:, 1:2], rs[:, 1, 0:N], ALU.mult, ALU.mult,
                    r=[("ps", b), ("rs", 1), "qkcol"], w=[("tmp", t0)])
                if sample:
                    cp("dve", kTs[:, p, 0:N], tmp[:, t0, 0:N], r=[("tmp", t0)], w=[("kTs", p)])
                else:
                    for hh in range(2):
                        dma("pool", kscr[2 * p + hh, :, tok0:tok0 + N], tmp[64 * hh:64 * hh + 64, t0, 0:N], f"kst{t0}_{hh}",
                            r=[("tmp", t0)], w=[("kscr", 2 * p + hh)])
                st8[("t", i)] = t0

        def stageC(i):
            which, slot, p, o = tasks[i]
            if which != "k":
                return
            t0 = st8[("t", i)]
            ko = 0
            bt = tbank()
            nblk = (N + 127) // 128
            for jb in range(nblk):
                nb = min(128, N - jb * 128)
                tr(ps[bt][0:nb, jb * 128:jb * 128 + 128], tmp[:, t0, jb * 128:jb * 128 + nb], identf, r=[("tmp", t0), "cstf"], w=[("ps", bt)])
            nb = min(128, N)
            cp("dve", kost[0:nb, ko, 0:nblk, :], ps[bt][0:nb, 0:nblk * 128].rearrange("p (j c) -> p j c", c=128),
               r=[("ps", bt)], w=[("kost", ko)])
            final_refs.append(dma("pool", o_k[tok0:tok0 + N, p * 128:(p + 1) * 128].rearrange("(j t) c -> t j c", t=nb),
                                  kost[0:nb, ko, 0:nblk, :], f"o_k{ko}", r=[("kost", ko)], w=[("o_k", ti, p)]))

        stageA(0)
        for i in range(len(tasks)):
            if i + 1 < len(tasks):
                stageA(i + 1)
            stageB(i)
            if i >= 1:
                stageC(i - 1)
        stageC(len(tasks) - 1)
        chk('a_qk')
        sv = [wload("inc", 4), wload("inc", 5)]
        nb = SL if sample else 128
        for jb in range(N // nb):
            st = rot("stg", 2)
            for half in range(2):
                s, W = sv[half]
                b = mmbank()
                for k in range(8):
                    mm(ps[b][0:nb, 0:512], H[:, k, jb * nb:(jb + 1) * nb], wr(s, W, k, 0, 512), start=(k == 0), stop=(k == 7),
                       r=[("wr", s), ("H", k)], w=[("ps", b)])
                if "vnoact" not in os.environ.get("KDBG", ""):
                    act(stg[0:nb, st, 512 * half:512 * half + 512], ps[b][0:nb, :], AF.Copy, r=[("ps", b)], w=[("stg", st)])
                if "vnodve" in os.environ.get("KDBG", ""):
                    pass
                elif sample:
                    cp("dve", Vnew[0:nb, jb, 512 * half:512 * half + 512], stg[0:nb, st, 512 * half:512 * half + 512], r=[("stg", st)], w=VK(14, 4))
                else:
                    cp("dve", V[0:nb, tok0 // 128 + jb, 512 * half:512 * half + 512], stg[0:nb, st, 512 * half:512 * half + 512],
                       r=[("stg", st)], w=[("V", tok0 // 128 + jb)])
            if "ovsp" in os.environ.get("KDBG", ""):
                final_refs.append(dma("sp", o_v[tok0 + jb * nb:tok0 + (jb + 1) * nb, :], stg[0:nb, st, :], f"stg{st}", r=[("stg", st)], w=[("o_v", ti, jb)]))
            elif "noov" in os.environ.get("KDBG", ""):
                pass
            else:
                final_refs.append(dma("pool", o_v[tok0 + jb * nb:tok0 + (jb + 1) * nb, :], stg[0:nb, st, :], f"so{st}", r=[("stg", st)], w=[("o_v", ti, jb)]))
        chk('a_v')
        s, W = wload("inc", 6)
        b = mmbank()
        for k in range(8):
            mm(ps[b][0:NH, 0:N], wr(s, W, k, 0, NH), H[:, k, 0:N], start=(k == 0), stop=(k == 7), r=[("wr", s), ("H", k)], w=[("ps", b)])
        act(nl[:, 0:N], ps[b][0:NH, 0:N], AF.Exp, scale=-1.0, bias=negb[:, 0:1], r=[("ps", b), "negb"], w=["nl"])
        act(nl[:, 0:N], nl[:, 0:N], AF.Ln, bias=onec[0:NH, 0:1], r=["nl", "onec"], w=["nl"])
        if sample:
            for q in range(NS):
                P.add("dve", lambda e, q=q: e.tensor_tensor_scan(out=nd[:, q * SL:(q + 1) * SL], data0=onesf[:, 0:SL], data1=nl[:, q * SL:(q + 1) * SL],
                                                               initial=0.0, op0=ALU.mult, op1=ALU.add), ["nl", "onesf"], ["nd"])
        else:
            P.add("dve", lambda e: e.tensor_tensor_scan(out=nd[:, 0:N], data0=onesf[:, 0:N], data1=nl[:, 0:N], initial=ndcarry[:, 0:1],
                                                      op0=ALU.mult, op1=ALU.add), ["nl", "onesf", "ndcarry"], ["nd"])
            cp("dve", ndcarry[:, :], nd[:, N - 1:N], r=["nd"], w=["ndcarry"])
        chk('a_f1')
        ts("dve", spl[:, 0, 0:N], nd[:, 0:N], -1.0, None, ALU.mult, r=["nd"], w=[("spl", 0)])
        stt(r1[:, 0:N], nd[:, 0:N], -1.0, spl[:, 0, 0:N], ALU.mult, ALU.subtract, r=["nd", ("spl", 0)], w=[("rs", 0)])
        cp("dve", spl[:, 1, 0:N], r1[:, 0:N], r=[("rs", 0)], w=[("spl", 1)])
        tt("dve", r1[:, 0:N], r1[:, 0:N], spl[:, 1, 0:N], ALU.subtract, r=[("rs", 0), ("spl", 1)], w=[("rs", 0)])
        cp("dve", spl[:, 2, 0:N], r1[:, 0:N], r=[("rs", 0)], w=[("spl", 2)])
        for h in range(NH):
            base = 0 if h % 2 else 64
            for r_ in range(3):
                dma("pool", Gq[base + r_:base + r_ + 1, h, 0:N], spl[h:h + 1, r_, 0:N], f"aug{h}_{r_}", r=[("spl", r_), ("G", h)], w=[("Gaug", h, r_)])
        chk('a_f2')
        if sample:
            for q in range(NS):
                bt = tbank()
                tr(ps[bt][0:SL, 0:NH], nl[:, q * SL:(q + 1) * SL], identf[0:NH, 0:NH], r=["nl", "cstf"], w=[("ps", bt)])
                tr(ps[bt][0:SL, NH:2 * NH], nd[:, q * SL:(q + 1) * SL], identf[0:NH, 0:NH], r=["nd", "cstf"], w=[("ps", bt)])
                ts("dve", lfst[0:SL, q, :], ps[bt][0:SL, 0:NH], -1.0, None, ALU.mult, r=[("ps", bt)], w=["lfst"])
                cp("dve", ndtoks[:, q, :], ps[bt][0:SL, NH:2 * NH], r=[("ps", bt)], w=["ndtoks"])
            final_refs.append(dma("pool", o_lf.rearrange("(q t) h -> t q h", t=SL), lfst[0:SL, 0:NS, :], "o_lf", r=["lfst"], w=[("o_lf", ti)]))
        else:
            for jb in range(N // 128):
                bt = tbank()
                tr(ps[bt][:, 0:NH], nl[:, jb * 128:(jb + 1) * 128], identf[0:NH, 0:NH], r=["nl", "cstf"], w=[("ps", bt)])
                tr(ps[bt][:, NH:2 * NH], nd[:, jb * 128:(jb + 1) * 128], identf[0:NH, 0:NH], r=["nd", "cstf"], w=[("ps", bt)])
                ts("dve", lfst[:, jb, :], ps[bt][:, 0:NH], -1.0, None, ALU.mult, r=[("ps", bt)], w=["lfst"])
                cp("dve", ndtok[:, tok0 // 128 + jb, :], ps[bt][:, NH:2 * NH], r=[("ps", bt)], w=[("ndtok", tok0 // 128 + jb)])
            final_refs.append(dma("pool", o_lf[tok0:tok0 + N, :].rearrange("(j t) h -> t j h", t=128), lfst[:, 0:N // 128, :], "o_lf",
                                  r=["lfst"], w=[("o_lf", ti)]))

        chk('a_f3')
        def head_blocks(h, keys, nq, qcol0, bo, bd):
            odd = h % 2
            p = h // 2
            rw = slice(64, 128) if odd else slice(0, 64)
            nkb = len(keys)
            qk = [("G", h)] + [("Gaug", h, r_) for r_ in range(3)]
            pend = []

            def pv(kb, et, c0, lv, vk, nbk):
                mm(ps[bo][:, c0:nq], lv, E[0:nbk, et, c0:nq], start=(kb == 0), stop=(kb == nkb - 1), r=[("E", et)] + vk, w=[("ps", bo)])
                mm(ps[bd][:, c0:nq], onesb[0:nbk, :], E[0:nbk, et, c0:nq], start=(kb == 0), stop=(kb == nkb - 1), r=[("E", et), "onesb"], w=[("ps", bd)])

            for kb, (lk, kk, bias, bk, lv, vk, dg, nbk) in enumerate(keys):
                c0 = 0 if dg is None else dg
                bs_ = (0, 1, 6)[rot("sb3", 3)]
                mm(ps[bs_][0:nbk, c0:nq], lk, Gq[:, h, qcol0 + c0:qcol0 + nq], start=True, stop=(dg is None), r=qk + kk, w=[("ps", bs_)])
                if dg is not None:
                    dgw = min(128, nq - c0)
                    mm(ps[bs_][0:nbk, c0:c0 + dgw], identb[0:nbk, 0:nbk], negmaskb[0:nbk, 0:dgw], start=False, stop=True,
                       r=["identb", "negmaskb"], w=[("ps", bs_)])
                et = rot("E", 3)
                act(E[0:nbk, et, c0:nq], ps[bs_][0:nbk, c0:nq], AF.Exp, bias=bias, r=[("ps", bs_)] + bk, w=[("E", et)])
                if len(pend) >= 2:
                    pv(*pend.pop(0))
                pend.append((kb, et, c0, lv, vk, nbk))
            while len(pend) > 1:
                pv(*pend.pop(0))
            pv(*pend.pop())
            ri = rot("tmp", 4)
            act(tmp[rw, ri, 0:nq], ps[bd][rw, 0:nq], AF.Ln, r=[("ps", bd)], w=[("tmp", ri)])
            act(tmp[rw, ri, 0:nq], tmp[rw, ri, 0:nq], AF.Exp, scale=-1.0, r=[("tmp", ri)], w=[("tmp", ri)])
            tt("dve", oT[rw, p, qcol0:qcol0 + nq], ps[bo][rw, 0:nq], tmp[rw, ri, 0:nq], ALU.mult, r=[("ps", bo), ("tmp", ri)], w=[("H", p)])

        if not sample:
            nk = tok0 + N
            for h in range(NH):
                odd = h % 2
                p = h // 2
                rw = slice(64, 128) if odd else slice(0, 64)
                dma("sp", kbuf[odd][rw, 0:nk], kscr[h, :, 0:nk], f"kb{odd}", r=[("kscr", h)], w=[("kb", odd)])
                keys = []
                for kb in range(nk // 128):
                    j = kb - tok0 // 128
                    keys.append((kbuf[odd][:, kb * 128:(kb + 1) * 128], [("kb", odd)], ndtok[:, kb, h:h + 1], [("ndtok", kb)],
                                 V[:, kb, p * 128:(p + 1) * 128], [("V", kb)], (j * 128 if j >= 0 else None), 128))
                bo, bd = (2, 3) if h % 2 == 0 else (4, 5)
                head_blocks(h, keys, N, 0, bo, bd)
        else:
            for q in range(NS):
                dma("sp", ltok[:, 0:NPB, :], clf[q, :, :].rearrange("(k s) h -> s k h", s=128), "ltok", w=VK(13, 1))
                for g4 in range((NPB + 3) // 4):
                    bt = tbank()
                    n4 = min(4, NPB - 4 * g4)
                    for kk in range(n4):
                        tr(ps[bt][0:NH, kk * 128:(kk + 1) * 128], ltok[:, 4 * g4 + kk, :], identf, r=VK(13, 1) + ["cstf"], w=[("ps", bt)])
                    ts("dve", nlc[0:NH, 512 * g4:512 * g4 + 128 * n4], ps[bt][0:NH, 0:128 * n4], -1.0, None, ALU.mult, r=[("ps", bt)], w=VK(8, 4))
                for c0 in range(0, PAST, TN):
                    cw = min(TN, PAST - c0)
                    P.add("dve", lambda e, c0=c0, cw=cw: e.tensor_tensor_scan(out=nlc[0:NH, c0:c0 + cw], data0=onesf[:, 0:cw], data1=nlc[0:NH, c0:c0 + cw],
                                                                           initial=(0.0 if c0 == 0 else nlc[0:NH, c0 - 1:c0]), op0=ALU.mult, op1=ALU.add),
                          VK(8, 4) + ["onesf"], VK(8, 4))
                cp("dve", r1[:, 0:1], nlc[0:NH, PAST - 1:PAST], r=VK(8, 4), w=[("rs", 0)])
                ts("dve", nlc[0:NH, 0:PAST], nlc[0:NH, 0:PAST], r1[:, 0:1], None, ALU.subtract, r=VK(8, 4) + [("rs", 0)], w=VK(8, 4))
                bt = tbank()
                for kb in range(NPB):
                    tr(ps[bt][:, kb * NH:(kb + 1) * NH], nlc[0:NH, kb * 128:(kb + 1) * 128], identf[0:NH, 0:NH], r=VK(8, 4) + ["cstf"], w=[("ps", bt)])
                cp("dve", biasc[:, 0:NPB, :], ps[bt][:, 0:NPB * NH].rearrange("p (k h) -> p k h", h=NH), r=[("ps", bt)], w=VK(12, 1))
                for p in range(8):
                    ci = p % 2
                    dma("pool", kc[ci][:, 0:NPB, :], ck[q, :, p * 128:(p + 1) * 128].rearrange("(k s) c -> s k c", s=128), f"kc{ci}", w=VK(2 * ci, 2))
                    dma("pool", vc[ci][:, 0:NPB, :], cv[q, :, p * 128:(p + 1) * 128].rearrange("(k s) c -> s k c", s=128), f"vc{ci}", w=VK(4 + 2 * ci, 2))
                    for g8 in range((NPB + 7) // 8):
                        n8 = min(8, NPB - 8 * g8)
                        bt = tbank()
                        pbf = ps[bt][:, :].bitcast(BF16)
                        for kk in range(n8):
                            tr(pbf[:, kk * 128:(kk + 1) * 128], kc[ci][:, 8 * g8 + kk, :], identb[:, :], r=VK(2 * ci, 2) + ["identb"], w=[("ps", bt)])
                        cp("dve", kbuf[0][0:64, 1024 * g8:1024 * g8 + 128 * n8], pbf[0:64, 0:128 * n8], r=[("ps", bt)], w=[("kb", 0)])
                        cp("dve", kbuf[1][64:128, 1024 * g8:1024 * g8 + 128 * n8], pbf[64:128, 0:128 * n8], r=[("ps", bt)], w=[("kb", 1)])
                    cp("dve", kbuf[0][0:64, PAST:PAST + SL], kTs[0:64, p, q * SL:(q + 1) * SL], r=[("kTs", p)], w=[("kb", 0)])
                    cp("dve", kbuf[1][64:128, PAST:PAST + SL], kTs[64:128, p, q * SL:(q + 1) * SL], r=[("kTs", p)], w=[("kb", 1)])
                    for hh in range(2):
                        h = 2 * p + hh
                        keys = []
                        for kb in range(NPB):
                            keys.append((kbuf[hh][:, kb * 128:(kb + 1) * 128], [("kb", hh)], biasc[:, kb, h:h + 1], VK(12, 1),
                                         vc[ci][:, kb, :], VK(4 + 2 * ci, 2), None, 128))
                        keys.append((kbuf[hh][:, PAST:PAST + SL], [("kb", hh)], ndtoks[:, q, h:h + 1], ["ndtoks"],
                                     Vnew[0:SL, q, p * 128:(p + 1) * 128], VK(14, 4), 0, SL))
                        bo, bd = (2, 3) if hh == 0 else (4, 5)
                        head_blocks(h, keys, SL, q * SL, bo, bd)
        chk('a_core')
        for half in range(2):
            s, W = wload("outc", half)
            for o in range(4):
                b = mmbank()
                proj_fm(s, W, o * 128, N, lambda k: oT[:, k, 0:N], lambda k: ("H", k), b)
                resid(half * 4 + o, b, N)

    def store_y(dst, N, ti):
        for b in range((N + 127) // 128):
            nb = min(128, N - b * 128)
            st = rot("stg", 2)
            for g in range(2):
                bank = tbank()
                for cc in range(4):
                    c = 4 * g + cc
                    tr(ps[bank][0:nb, cc * 128:(cc + 1) * 128], xT[:, c, b * 128:b * 128 + nb], identf, r=[("xT", c), "cstf"], w=[("ps", bank)])
                act(stg[0:nb, st, 512 * g:512 * g + 512], ps[bank][0:nb, :], AF.Copy, r=[("ps", bank)], w=[("stg", st)])
            final_refs.append(dma("pool", dst[b * 128:b * 128 + nb, :], stg[0:nb, st, :], f"so{st}", r=[("stg", st)], w=[("o_y", ti, b)]))

    def chk(name):
        if stop == name:
            raise _Stop()

    def state_out(src4, nrow, dst, keys, tag):
        nch = src4.shape[1]
        for g in range((nch + 3) // 4):
            n4 = min(4, nch - 4 * g)
            bank = tbank()
            st = rot("stg", 2)
            for cc in range(n4):
                tr(ps[bank][0:nrow, cc * 128:(cc + 1) * 128], src4[:, 4 * g + cc, :, :].rearrange("p s r -> p (s r)"), identf, r=list(keys) + ["cstf"], w=[("ps", bank)])
            act(stg[0:nrow, st, 0:128 * n4], ps[bank][0:nrow, 0:128 * n4], AF.Copy, r=[("ps", bank)], w=[("stg", st)])
            final_refs.append(dma("pool", dst[:, 512 * g:512 * g + 128 * n4], stg[0:nrow, st, 0:128 * n4], f"so{st}", r=[("stg", st)], w=[(tag, g)]))

    try:
        chk("init")
        for ti in range(NT):
            load_x(xp[ti * TN:(ti + 1) * TN, :], TN)
            chk("load")
            mixer_ab(False, TN, 1, TN)
            chk("mixer")
            ffn(0, False, TN, 1, TN)
            chk("ffn0")
            attn(False, ti, TN)
            chk("attn")
            ffn(1, False, TN, 1, TN)
            store_y(o_yp[ti * TN:(ti + 1) * TN, :], TN, ti)
            chk("tile")
        load_x(xs, NSTOK)
        mixer_ab(True, NSTOK, NS, SL)
        chk("smixer")
        ffn(0, True, NSTOK, NS, SL)
        chk("sffn0")
        attn(True, NT, NSTOK)
        chk("sattn")
        ffn(1, True, NSTOK, NS, SL)
        store_y(o_ys, NSTOK, NT)
        chk("stile")
        state_out(pbhP[:, :, :, :], 2, o_cbp, ["pbhP"], "o_cbp")
        state_out(pbhS[:, :, :, :], 2 * NS, o_cbs, ["pbhS"], "o_cbs")
        for l in range(2):
            state_out(zhP[:, l, :, :, :], 2, o_ffp[l, :, :], [("zhP", l)], f"o_ffp{l}")
            state_out(zhS[:, l, :, :, :], 2 * NS, o_ffs[l, :, :], [("zhS", l)], f"o_ffs{l}")
    except _Stop:
        if "nostore" not in os.environ.get("KDBG", ""):
            store_y(o_yp[0:TN, :], TN, 99)
    P.add("sp", lambda e: None, extra=final_refs)
    P.emit(nc, es)
    es.close()
    return nc


def make_consts():
    c = np.zeros((128, 384), np.float32)
    c[:, 0:128] = np.eye(128, dtype=np.float32)
    s = np.arange(128)[:, None]
    t = np.arange(128)[None, :]
    c[:, 128:256] = np.where(s <= t, 0.0, NEGM).astype(np.float32)
    c[:, 256:384] = 1.0
    return c


def core_inputs(inp, i, NS, n_cores):
    a = lambda x: np.ascontiguousarray(np.asarray(x, dtype=np.float32))
    sl = slice(NS * i, NS * (i + 1))
    m = {
        "xp": a(inp["x_prompt"][i]), "xs": a(inp["x_sample"][sl]).reshape(NS * SL, D),
        "st_cb": a(inp["state_conv_b"][0, sl]).reshape(NS * 2, 512),
        "st_ffn": a(inp["state_ffn"][:, sl]).reshape(2, NS * 2, DUP),
        "ck": a(inp["cache_k"][0, sl]).reshape(NS, -1, D), "cv": a(inp["cache_v"][0, sl]).reshape(NS, -1, D),
        "clf": a(inp["cache_logf"][0, sl]),
        "norm_mix": a(inp["norm_mix"]), "norm_ffn": a(inp["norm_ffn"]), "w_in_ab": a(inp["w_in_ab"][0]),
        "sgu_norm": a(inp["sgu_norm"][0]), "w_spatial": a(inp["w_spatial"][0]), "b_spatial": a(inp["b_spatial"][0]),
        "conv_b": a(inp["conv_b"][0]), "w_out_ab": a(inp["w_out_ab"][0]), "w_in_c": a(inp["w_in_c"][0]),
        "b_forget": a(inp["b_forget"][0]).reshape(NH, 1), "q_norm": a(inp["q_norm"][0]).reshape(64, 1),
        "k_norm": a(inp["k_norm"][0]).reshape(64, 1), "w_out_c": a(inp["w_out_c"][0]), "w_up": a(inp["w_up"]),
        "conv_ffn": a(inp["conv_ffn"]), "w_down": a(inp["w_down"]), "cst": make_consts(),
    }
    return m


def assemble(results, B, T, NS):
    cat = lambda k: np.stack([np.asarray(r[k], dtype=np.float32) for r in results])
    DB = B * NS
    yp = cat("o_yp")
    ys = cat("o_ys").reshape(DB, SL, D)
    cbp = cat("o_cbp")[None]
    cbs = cat("o_cbs").reshape(1, DB, 2, 512)
    gvs = cat("o_gv").reshape(1, DB, SL, 512)
    kp = cat("o_kp").reshape(1, B, T, NH, 64)
    vp = cat("o_vp").reshape(1, B, T, NH, 64)
    lfp = cat("o_lfp")[None]
    ks = cat("o_ks").reshape(1, DB, SL, NH, 64)
    vs = cat("o_vs").reshape(1, DB, SL, NH, 64)
    lfs = cat("o_lfs").reshape(1, DB, SL, NH)
    ffp = np.ascontiguousarray(cat("o_ffp").transpose(1, 0, 2, 3))
    ffs = np.ascontiguousarray(cat("o_ffs").reshape(B, 2, NS, 2, DUP).transpose(1, 0, 2, 3, 4)).reshape(2, DB, 2, DUP)
    return (yp, ys, cbp, cbs, gvs, kp, vp, lfp, ks, vs, lfs, ffp, ffs)


def kernel(**inputs):
    B, T, _ = inputs["x_prompt"].shape
    DB = inputs["x_sample"].shape[0]
    NS = DB // B
    PAST = inputs["cache_k"].shape[2]
    nc = build(T, NS, PAST)
    in_maps = [core_inputs(inputs, i, NS, B) for i in range(B)]
    res = run_bass_kernel_spmd(nc, in_maps, core_ids=list(range(B)))
    return assemble(res.results, B, T, NS)
```

```python
import os
import sys
from contextlib import ExitStack

import numpy as np
import concourse.bass as bass
import concourse.mybir as mybir
from concourse.bass_utils import run_bass_kernel_spmd

F32 = mybir.dt.float32
BF16 = mybir.dt.bfloat16
AF = mybir.ActivationFunctionType
ALU = mybir.AluOpType

D = 1024
DUP = 5632
DFF = 2816
NH = 16
TN = 512
SL = 16
EPS = 1e-6
NEGM = -30000.0
GELU = AF.Gelu_apprx_tanh
ENGS = ("pe", "act", "dve", "pool", "sp")
EPOCH = 20000


class Op:
    __slots__ = ("fn", "waits", "dma", "sig", "cnt", "tag")

    def __init__(self, fn, waits, dma):
        self.fn, self.waits, self.dma, self.sig, self.cnt = fn, waits, dma, False, 0
        f = sys._getframe(2)
        tg = []
        while f is not None and len(tg) < 4:
            tg.append(str(f.f_lineno))
            f = f.f_back
        self.tag = "<".join(tg)


class Prog:
    def __init__(self):
        self.ops = {e: [] for e in ENGS}
        self.lastw, self.readers, self.dcount = {}, {}, {}

    def _need(self, eng, dma, ref, raw):
        if ref[0] == "d" or ref[1] != eng or dma is not None:
            return True
        if eng == "pe":
            return False
        return True

    def add(self, eng, fn, r=(), w=(), dma=None, extra=()):
        idx = len(self.ops[eng])
        if dma is not None:
            c = self.dcount.get(dma, 0) + 1
            self.dcount[dma] = c
            me = ("d", dma, c)
        else:
            me = ("e", eng, idx)
        waits = set(extra)
        for b in r:
            lw = self.lastw.get(b)
            if lw is not None and self._need(eng, dma, lw, True):
                waits.add(lw)
        for b in w:
            lw = self.lastw.get(b)
            if lw is not None and self._need(eng, dma, lw, False):
                waits.add(lw)
            for rr in self.readers.get(b, {}).values():
                if rr != me and self._need(eng, dma, rr, False):
                    waits.add(rr)
        for b in w:
            self.lastw[b] = me
            self.readers[b] = {}
        for b in r:
            self.readers.setdefault(b, {})[me[:2]] = me
        self.ops[eng].append(Op(fn, waits, dma))
        return me

    def emit(self, nc, es):
        for e in ENGS:
            for op in self.ops[e]:
                for ref in op.waits:
                    if ref[0] == "e":
                        self.ops[ref[1]][ref[2]].sig = True
        esem = {}
        for e in ENGS:
            c = 0
            for op in self.ops[e]:
                if op.sig:
                    c += 1
                    op.cnt = c
            esem[e] = [es.enter_context(nc.semaphore(f"s_{e}_{k}")) for k in range(c // EPOCH + 1)]
        dsem = {k: es.enter_context(nc.semaphore(f"d_{k}")) for k in self.dcount}
        block = es.enter_context(nc.Block())
        engobj = {"pe": block.tensor, "act": block.scalar, "dve": block.vector, "pool": block.gpsimd, "sp": block.sync}

        def body_for(ename):
            def body(e):
                waited = {}
                for op in self.ops[ename]:
                    for ref in sorted(op.waits, key=str):
                        if ref[0] == "e":
                            c = self.ops[ref[1]][ref[2]].cnt
                            k = ("e", ref[1])
                            if waited.get(k, 0) >= c:
                                continue
                            waited[k] = c
                            ep = (c - 1) // EPOCH
                            e.wait_ge(esem[ref[1]][ep], c - ep * EPOCH)
                        else:
                            k = ("d", ref[1])
                            if waited.get(k, 0) >= ref[2]:
                                continue
                            waited[k] = ref[2]
                            e.wait_ge(dsem[ref[1]], 16 * ref[2])
                    try:
                        ins = op.fn(e)
                    except Exception as ex:
                        raise RuntimeError(f"emit failed for op at line {op.tag} on {ename}: {str(ex)[:300]}") from None
                    if ins is None:
                        continue
                    if op.dma is not None:
                        ins.then_inc(dsem[op.dma], 16)
                    elif op.sig:
                        ep = (op.cnt - 1) // EPOCH
                        ins.then_inc(esem[ename][ep], 1)
            return body

        for ename in ENGS:
            engobj[ename](body_for(ename))


class _Stop(Exception):
    pass


def build(T, NS, PAST, stop=None):
    NT = T // TN
    NBLK = T // 128
    NSTOK = NS * SL
    NPB = PAST // 128
    VBLK = max(NBLK, 18)
    KW = max(T, PAST + SL)
    nc = bass.Bass("TRN2", target_bir_lowering=False)
    P = Prog()
    es = ExitStack()

    def din(name, shape):
        return nc.dram_tensor(name, list(shape), F32, kind="ExternalInput").ap()

    def dout(name, shape):
        return nc.dram_tensor(name, list(shape), F32, kind="ExternalOutput").ap()

    def dscr(name, shape):
        return nc.dram_tensor(name, list(shape), BF16, kind="Internal").ap()

    xp = din("xp", [T, D]); xs = din("xs", [NSTOK, D])
    st_cb = din("st_cb", [NS * 2, 512]); st_ffn = din("st_ffn", [2, NS * 2, DUP])
    ck = din("ck", [NS, PAST, D]); cv = din("cv", [NS, PAST, D]); clf = din("clf", [NS, PAST, NH])
    norm_mix = din("norm_mix", [2, D]); norm_ffn = din("norm_ffn", [2, D])
    w_in_ab = din("w_in_ab", [D, 2560]); sgu_norm = din("sgu_norm", [4, 128])
    w_spatial = din("w_spatial", [4, 128, 128]); b_spatial = din("b_spatial", [4, 128])
    conv_b = din("conv_b", [3, 512]); w_out_ab = din("w_out_ab", [D, D])
    w_in_c = din("w_in_c", [D, 3088]); b_forget = din("b_forget", [NH, 1])
    q_norm = din("q_norm", [64, 1]); k_norm = din("k_norm", [64, 1]); w_out_c = din("w_out_c", [D, D])
    w_up = din("w_up", [2, D, DUP]); conv_ffn = din("conv_ffn", [2, 3, DUP]); w_down = din("w_down", [2, DFF, D])
    cst = din("cst", [128, 384])

    o_yp = dout("o_yp", [T, D]); o_ys = dout("o_ys", [NSTOK, D])
    o_cbp = dout("o_cbp", [2, 512]); o_cbs = dout("o_cbs", [NS * 2, 512]); o_gv = dout("o_gv", [NSTOK, 512])
    o_kp = dout("o_kp", [T, D]); o_vp = dout("o_vp", [T, D]); o_lfp = dout("o_lfp", [T, NH])
    o_ks = dout("o_ks", [NSTOK, D]); o_vs = dout("o_vs", [NSTOK, D]); o_lfs = dout("o_lfs", [NSTOK, NH])
    o_ffp = dout("o_ffp", [2, 2, DUP]); o_ffs = dout("o_ffs", [2, NS * 2, DUP])
    kscr = dscr("kscr", [NH, 64, T])

    def sb(name, shape, dt=F32):
        return es.enter_context(nc.sbuf_tensor(name, list(shape), dt))

    cstf = sb("cstf", [128, 384])
    identf = cstf[:, 0:128]; negmask = cstf[:, 128:256]; negmaskb = cstf[:, 256:320].bitcast(BF16)
    identb = sb("identb", [128, 128], BF16); onesb = sb("onesb", [128, 128], BF16); blkones = sb("blkones", [128, 128], BF16)
    col = sb("col", [128, 320]); qkcol = sb("qkcol", [128, 2]); negb = sb("negb", [NH, 1])
    epsc = sb("epsc", [128, 1]); onec = sb("onec", [128, 1]); onesf = sb("onesf", [NH, TN])
    stg = sb("stg", [128, 2, 1024])
    xT = sb("xT", [128, 8, TN]); H = sb("H", [128, 8, TN], BF16); G = sb("G", [128, 22, TN], BF16)
    V = sb("V", [128, VBLK, 1024], BF16)
    kbuf = [sb("kbufE", [128, KW], BF16), sb("kbufO", [128, KW], BF16)]
    wring = sb("wring", [128, 3, 4096], BF16)
    zbuf = sb("zbuf", [128, 2, 2 + TN]); tmp = sb("tmp", [128, 4, TN]); rs = sb("rs", [128, 2, TN])
    pbhP = sb("pbhP", [128, 4, 1, 2]); pbhS = sb("pbhS", [128, 4, NS, 2])
    zhP = sb("zhP", [128, 2, 44, 1, 2]); zhS = sb("zhS", [128, 2, 44, NS, 2])
    E = sb("E", [128, 3, TN], BF16)
    gv = sb("gv", [128, 1, 512]); vnf = sb("vnf", [128, 512]); vtok = sb("vtok", [128, 1, 512], BF16)
    vnb = sb("vnb", [128, 512]); vss = sb("vss", [128, 8])
    WmT = sb("WmT", [128, 4, 128], BF16); bsb = sb("bsb", [1, 4, 128], BF16)
    ndtok = sb("ndtok", [128, NBLK, NH]); ndtoks = sb("ndtoks", [SL, NS, NH])
    nl = sb("nl", [NH, TN]); nd = sb("nd", [NH, TN]); ndcarry = sb("ndcarry", [NH, 1]); r1 = rs[0:NH, 0, :]
    spl = sb("spl", [NH, 3, TN], BF16)
    lfst = sb("lfst", [128, 4, NH]); kost = sb("kost", [128, 1, 4, 128]); kst = None
    sqh = E; kTs = sb("kTs", [128, 8, 64], BF16)

    ps = [es.enter_context(nc.psum_tensor(f"ps{i}", [128, 512], F32)) for i in range(8)]
    Gq = G
    sq = G[:, 14:22, :]

    def vview(b0, nb, dt, pat=None, **kw):
        a = V[:, b0:b0 + nb, :].rearrange("p b c -> p (b c)")
        if dt == F32:
            a = a.bitcast(F32)
        return a if pat is None else a.rearrange(pat, **kw)

    kc = [vview(0, 2, BF16, "p (k c) -> p k c", c=128), vview(2, 2, BF16, "p (k c) -> p k c", c=128)]
    vc = [vview(4, 2, BF16, "p (k c) -> p k c", c=128), vview(6, 2, BF16, "p (k c) -> p k c", c=128)]
    nlc = vview(8, 4, F32)
    biasc = vview(12, 1, F32, "p (k h) -> p k h", h=NH)
    ltok = vview(13, 1, F32, "p (k h) -> p k h", h=NH)
    Vnew = vview(14, 4, BF16, "p (s c) -> p s c", c=1024)
    VK = lambda b0, nb: [("V", b) for b in range(b0, b0 + nb)]

    def mm(out, lhsT, rhs, start=True, stop=True, r=(), w=()):
        return P.add("pe", lambda e: e.matmul(out, lhsT=lhsT, rhs=rhs, start=start, stop=stop), r, w)

    def tr(out, in_, ident, r=(), w=()):
        return P.add("pe", lambda e: e.transpose(out=out, in_=in_, identity=ident), r, w)

    def act(out, in_, func, r=(), w=(), **kw):
        return P.add("act", lambda e: e.activation(out=out, in_=in_, func=func, **kw), r, w)

    def tt(eng, out, in0, in1, op, r=(), w=()):
        return P.add(eng, lambda e: e.tensor_tensor(out=out, in0=in0, in1=in1, op=op), r, w)

    def ts(eng, out, in0, s1, s2, op0, op1=ALU.bypass, r=(), w=()):
        return P.add(eng, lambda e: e.tensor_scalar(out=out, in0=in0, scalar1=s1, scalar2=s2, op0=op0, op1=op1), r, w)

    def stt(out, in0, scalar, in1, op0, op1, r=(), w=()):
        return P.add("dve", lambda e: e.scalar_tensor_tensor(out=out, in0=in0, scalar=scalar, in1=in1, op0=op0, op1=op1), r, w)

    def cp(eng, out, in_, r=(), w=()):
        return P.add(eng, lambda e: e.tensor_copy(out=out, in_=in_), r, w)

    def recip(out, in_, r=(), w=()):
        return P.add("dve", lambda e: e.reciprocal(out=out, in_=in_), r, w)

    def mset(eng, ap, val, w=()):
        return P.add(eng, lambda e: e.memset(ap, val), (), w)

    def dma(q, out, in_, key, r=(), w=(), **kw):
        return P.add(q, lambda e: e.dma_start(out=out, in_=in_, **kw), r, w, dma=key)

    ctr = {"mm": 0, "tb": 0, "stg": 0, "wr": 0, "zb": 0, "tmp": 0, "E": 0, "sb": 0, "ko": 0, "sb3": 0, "ss": 0, "es": 0}

    def rot(name, n, base=0):
        v = ctr[name]
        ctr[name] = (v + 1) % n
        return base + v

    mmbank = lambda: rot("mm", 4)
    tbank = lambda: rot("tb", 2, 6)
    final_refs = []

    def v3(ap, nseq, L):
        return ap if nseq == 1 and False else ap.rearrange("p (s l) -> p s l", s=nseq)

    groups = {}

    def defgroup(name, src, KC, slots):
        wmax = max(sum(c1 - c0 for c0, c1 in s) for s in slots)
        scr = dscr("w_" + name, [len(slots), 128, KC * wmax])
        groups[name] = (scr, KC, slots, wmax)
        srcv = src.rearrange("(k p) n -> p k n", p=128)
        last = None
        for si, segs in enumerate(slots):
            wtot = sum(c1 - c0 for c0, c1 in segs)
            o = 0
            for c0, c1 in segs:
                dst = scr[si, :, 0:KC * wtot].rearrange("p (k w) -> p k w", k=KC)[:, :, o:o + c1 - c0]
                last = dma("pool", dst, srcv[:, :, c0:c1], "cast_" + name, w=[("scr", name, si, o)])
                o += c1 - c0
        for si, segs in enumerate(slots):
            o = 0
            for c0, c1 in segs:
                P.lastw[("scr", name, si, o)] = last
                o += c1 - c0

    def wload(name, si):
        scr, KC, slots, wmax = groups[name]
        segs = slots[si]
        wtot = sum(c1 - c0 for c0, c1 in segs)
        s = rot("wr", 3)
        keys, o = [], 0
        for c0, c1 in segs:
            keys.append(("scr", name, si, o)); o += c1 - c0
        dma("sp", wring[:, s, 0:KC * wtot], scr[si, :, 0:KC * wtot], f"wr{s}", r=keys, w=[("wr", s)])
        return s, wtot

    def wr(s, W, k, c0, m):
        return wring[:, s, k * W + c0:k * W + c0 + m]

    rowsrc = [(norm_mix.rearrange("l (c p) -> (l c) p", p=128), 16), (norm_ffn.rearrange("l (c p) -> (l c) p", p=128), 16),
              (conv_b.rearrange("r (c p) -> (r c) p", p=128), 12), (conv_ffn.rearrange("l r (c p) -> (l r c) p", p=128), 264)]
    CM = lambda l, c: l * 8 + c
    CF = lambda l, c: 16 + l * 8 + c
    CB = lambda r_, c: 32 + r_ * 4 + c
    CW = lambda l, r_, ch: 44 + (l * 3 + r_) * 44 + ch
    rowt = tmp[:, 1:4, 0:128]
    ikeys = []

    def idma(out, in_, key):
        ikeys.append(key)
        dma("sp", out, in_, "init", w=[key])

    idma(cstf[:, :], cst[:, :], "cstf")
    row0 = 0
    for src, n in rowsrc:
        done = 0
        while done < n:
            t_i, off = divmod(row0 + done, 128)
            m = min(n - done, 128 - off)
            idma(rowt[off:off + m, t_i, :], src[done:done + m, :], ("rowt", t_i, off))
            done += m
        row0 += n
    for j, srcn in enumerate((q_norm, k_norm)):
        for h2 in range(2):
            idma(qkcol[64 * h2:64 * h2 + 64, j:j + 1], srcn[:, :], ("qkcol", j, h2))
    idma(negb[:, :], b_forget[:, :], "negb")
    idma(stg[0:1, 0, 0:512], sgu_norm.rearrange("g c -> (g c)").rearrange("(o n) -> o n", o=1), ("stg", 0))
    idma(stg[0:1, 0, 512:1024], b_spatial.rearrange("g c -> (g c)").rearrange("(o n) -> o n", o=1), ("stgi", 0))
    for g in range(4):
        idma(stg[:, 1, 128 * g:128 * g + 128], w_spatial[g, :, :], ("stgi", 1, g))
    idma(stg[0:2 * NS, 1, 512:1024], st_cb[:, :], ("stg", 1))
    for k_ in ikeys:
        P.lastw[k_] = ("d", "init", P.dcount["init"])
    INIT = [("stg", 0), ("stg", 1)]

    mset("pool", G[:, :, :], 0.0, w=[("G", c) for c in range(22)])
    mset("pool", kbuf[0][64:128, :], 0.0, w=[("kb", 0)])
    mset("pool", kbuf[0][64:67, :], 1.0, w=[("kb", 0)])
    mset("pool", kbuf[1][0:64, :], 0.0, w=[("kb", 1)])
    mset("pool", kbuf[1][0:3, :], 1.0, w=[("kb", 1)])
    mset("pool", onesb[:, :], 1.0, w=["onesb"])
    mset("pool", blkones[:, :], 0.0, w=["blkones"])
    mset("pool", blkones[0:64, 0:64], 1.0, w=["blkones"])
    mset("pool", blkones[64:128, 64:128], 1.0, w=["blkones"])
    mset("pool", epsc[:, :], EPS, w=["epsc"])
    mset("pool", onec[:, :], 1.0, w=["onec"])
    mset("pool", onesf[:, :], 1.0, w=["onesf"])
    mset("pool", ndcarry[:, :], 0.0, w=["ndcarry"])
    mset("pool", pbhP[:, :, :, :], 0.0, w=["pbhP"])
    mset("pool", zhP[:, :, :, :, :], 0.0, w=[("zhP", l) for l in range(2)])
    mset("pool", tmp[0:1, 0, 0:128], 1.0, w=[("tmp", 0)])
    cp("dve", identb[:, :], identf, r=["cstf"], w=["identb"])
    cp("dve", negmaskb, negmask, r=["cstf"], w=["negmaskb"])

    defgroup("inab", w_in_ab, 8, [[(0, 512)], [(512, 1024)]] +
             [[(1536 + 128 * c, 1664 + 128 * c), (2048 + 128 * c, 2176 + 128 * c), (1024 + 128 * c, 1152 + 128 * c)] for c in range(4)])
    defgroup("outab", w_out_ab, 8, [[(0, 512)], [(512, 1024)]])
    defgroup("up0", w_up[0], 8, [[(256 * j, 256 * j + 256), (DFF + 256 * j, DFF + 256 * j + 256)] for j in range(11)])
    defgroup("down0", w_down[0], 22, [[(128 * o, 128 * o + 128)] for o in range(8)])
    defgroup("inc", w_in_c, 8, [[(512 * j, 512 * j + 512)] for j in range(6)] + [[(3072, 3088)]])
    defgroup("outc", w_out_c, 8, [[(0, 512)], [(512, 1024)]])
    defgroup("up1", w_up[1], 8, [[(256 * j, 256 * j + 256), (DFF + 256 * j, DFF + 256 * j + 256)] for j in range(11)])
    defgroup("down1", w_down[1], 22, [[(128 * o, 128 * o + 128)] for o in range(8)])

    for t_i in range(3):
        nr = min(128, row0 - 128 * t_i)
        rk = [k_ for k_ in ikeys if isinstance(k_, tuple) and k_[0] == "rowt" and k_[1] == t_i]
        tr(ps[0][:, 0:nr], rowt[0:nr, t_i, :], identf[0:nr, 0:nr], r=rk + ["cstf", ("tmp", t_i + 1)], w=[("ps", 0)])
        cp("dve", col[:, 128 * t_i:128 * t_i + nr], ps[0][:, 0:nr], r=[("ps", 0)], w=["col"])
    ts("dve", qkcol[:, 0:1], qkcol[:, 0:1], 0.125, None, ALU.mult, r=[("qkcol", 0, 0), ("qkcol", 0, 1), ("qkcol", 1, 0), ("qkcol", 1, 1)], w=["qkcol"])
    ts("dve", negb[:, :], negb[:, :], -1.0, None, ALU.mult, r=["negb"], w=["negb"])
    mm(ps[1][:, 0:512], tmp[0:1, 0, 0:128], stg[0:1, 0, 0:512], r=[("stg", 0), ("tmp", 0)], w=[("ps", 1)])
    cp("dve", vnb[:, :], ps[1][:, 0:512], r=[("ps", 1)], w=["vnb"])
    cp("dve", bsb[0:1, :, :], stg[0:1, 0, 512:1024].rearrange("p (g c) -> p g c", g=4), r=[("stgi", 0), ("stg", 0)], w=["bsb"])
    for g in range(4):
        tr(ps[2][:, 128 * g:128 * g + 128], stg[:, 1, 128 * g:128 * g + 128], identf, r=[("stgi", 1, g), ("stg", 1), "cstf"], w=[("ps", 2)])
    cp("dve", WmT[:, :, :], ps[2][:, :].rearrange("p (g c) -> p g c", g=4), r=[("ps", 2)], w=["WmT"])
    mset("dve", WmT[64:128, :, 0:64], 0.0, w=["WmT"])
    for cc in range(4):
        tr(ps[3][:, cc * 2 * NS:(cc + 1) * 2 * NS], stg[0:2 * NS, 1, 512 + 128 * cc:512 + 128 * cc + 128], identf[0:2 * NS, 0:2 * NS],
           r=[("stg", 1), "cstf"], w=[("ps", 3)])
    cp("dve", pbhS[:, :, :, :].rearrange("p c s r -> p c (s r)"), ps[3][:, 0:8 * NS].rearrange("p (c x) -> p c x", c=4),
       r=[("ps", 3)], w=["pbhS"])
    for l in range(2):
        for g in range(11):
            st = rot("stg", 2)
            b = mmbank()
            dma("sp", stg[0:2 * NS, st, 0:512], st_ffn[l, :, 512 * g:512 * g + 512], f"stg{st}", w=[("stg", st)])
            for cc in range(4):
                tr(ps[b][:, cc * 2 * NS:(cc + 1) * 2 * NS], stg[0:2 * NS, st, 128 * cc:128 * cc + 128], identf[0:2 * NS, 0:2 * NS],
                   r=[("stg", st), "cstf"], w=[("ps", b)])
            cp("dve", zhS[:, l, 4 * g:4 * g + 4, :, :].rearrange("p c s r -> p c (s r)"),
               ps[b][:, 0:8 * NS].rearrange("p (c x) -> p c x", c=4), r=[("ps", b)], w=[("zhS", l)])

    def load_x(src, N):
        for b in range((N + 127) // 128):
            nb = min(128, N - b * 128)
            st = rot("stg", 2)
            dma("sp", stg[0:nb, st, :], src[b * 128:b * 128 + nb, :], f"stg{st}", w=[("stg", st)])
            for g in range(2):
                bank = tbank()
                for cc in range(4):
                    c = 4 * g + cc
                    tr(ps[bank][:, cc * 128:cc * 128 + nb], stg[0:nb, st, c * 128:(c + 1) * 128], identf[0:nb, 0:nb],
                       r=[("stg", st), "cstf"], w=[("ps", bank)])
                act(xT[:, 4 * g:4 * g + 4, b * 128:b * 128 + nb], ps[bank][:, :].rearrange("p (c t) -> p c t", c=4)[:, :, 0:nb], AF.Copy,
                    r=[("ps", bank)], w=[("xT", 4 * g + cc) for cc in range(4)])

    def norm(gc0, N):
        for c in range(8):
            act(sq[:, c, 0:N], xT[:, c, 0:N], AF.Square, r=[("xT", c)], w=[("G", 14 + c)])
            mm(ps[5][:, 0:N], onesb[:, :], sq[:, c, 0:N], start=(c == 0), stop=(c == 7), r=[("G", 14 + c), "onesb"], w=[("ps", 5)])
        act(rs[:, 0, 0:N], ps[5][:, 0:N], AF.Ln, scale=1.0 / D, bias=epsc[:, 0:1], r=[("ps", 5), "epsc"], w=[("rs", 0)])
        act(rs[:, 0, 0:N], rs[:, 0, 0:N], AF.Exp, scale=-0.5, r=[("rs", 0)], w=[("rs", 0)])
        for c in range(8):
            stt(H[:, c, 0:N], xT[:, c, 0:N], col[:, gc0 + c:gc0 + c + 1], rs[:, 0, 0:N], ALU.mult, ALU.mult,
                r=[("xT", c), ("rs", 0), "col"], w=[("H", c)])

    def proj_fm(s, W, c0, N, rhs_of, rkeys, bank):
        for k in range(8):
            mm(ps[bank][:, 0:N], wr(s, W, k, c0, 128), rhs_of(k), start=(k == 0), stop=(k == 7),
               r=[("wr", s), rkeys(k)], w=[("ps", bank)])

    def resid(oc, bank, N):
        tt("dve", xT[:, oc, 0:N], xT[:, oc, 0:N], ps[bank][:, 0:N], ALU.add, r=[("xT", oc), ("ps", bank)], w=[("xT", oc)])

    def conv3(zb, zkey, cur, curkeys, A, akey, l_cols, nseq, L):
        w0, w1, w2 = l_cols
        act(A, cur, AF.Copy, scale=col[:, w2:w2 + 1], r=list(curkeys) + ["col"], w=[akey])
        stt(A, zb[:, :, 1:1 + L], col[:, w1:w1 + 1], A, ALU.mult, ALU.add, r=[zkey, akey, "col"], w=[akey])
        stt(A, zb[:, :, 0:L], col[:, w0:w0 + 1], A, ALU.mult, ALU.add, r=[zkey, akey, "col"], w=[akey])

    def mixer_ab(sample, N, nseq, L):
        pbh = pbhS if sample else pbhP
        pbk = "pbhS" if sample else "pbhP"
        norm(CM(0, 0), N)
        uT = G[:, 0:4, :]
        ycat = G[:, 4:12, :]
        s, W = wload("inab", 0)
        for oc in range(4):
            b = mmbank()
            proj_fm(s, W, oc * 128, N, lambda k: H[:, k, 0:N], lambda k: ("H", k), b)
            act(uT[:, oc, 0:N], ps[b][:, 0:N], GELU, r=[("ps", b)], w=[("G", oc)])
        s, W = wload("inab", 1)
        nb = SL if sample else 128

        def gview(c0, nch, dt):
            a = G[:, c0:c0 + nch, :].rearrange("p c t -> p (c t)")
            return a.bitcast(F32) if dt == F32 else a

        gvb = [(gv[:, 0, :], [("gv", 0)]), (gview(12, 2, F32), [("G", 12), ("G", 13)])]
        vnfb = [(vnf, ["vnf"]), (gview(14, 2, F32), [("G", 14), ("G", 15)])]
        vtokb = [(vtok[:, 0, :], [("vtok", 0)]), (gview(16, 1, BF16), [("G", 16)])]
        nj = N // nb
        pbank = {}

        def vproj(j):
            b = mmbank()
            pbank[j] = b
            for k in range(8):
                mm(ps[b][0:nb, 0:512], H[:, k, j * nb:(j + 1) * nb], wr(s, W, k, 0, 512), start=(k == 0), stop=(k == 7),
                   r=[("wr", s), ("H", k)], w=[("ps", b)])

        def vchain(j):
            b = pbank[j]
            gvt, gk = gvb[j % 2]
            vnt, vk_ = vnfb[j % 2]
            vtt, tk = vtokb[j % 2]
            vs = vss[:, 0:8] if j % 2 == 0 else None
            act(gvt[0:nb, :], ps[b][0:nb, :], GELU, r=[("ps", b)], w=gk)
            for g in range(4):
                act(vnt[0:nb, 128 * g:128 * g + 128], gvt[0:nb, 128 * g:128 * g + 128], AF.Square, accum_out=vss[0:nb, g:g + 1],
                    r=gk, w=vk_ + ["vss"])
            act(vss[0:nb, 4:8], vss[0:nb, 0:4], AF.Sqrt, scale=1.0 / 128, bias=epsc[0:nb, 0:1], r=["vss", "epsc"], w=["vss2"])
            recip(vss[0:nb, 4:8], vss[0:nb, 4:8], r=["vss2"], w=["vss2"])
            for g in range(4):
                stt(vnt[0:nb, 128 * g:128 * g + 128], gvt[0:nb, 128 * g:128 * g + 128], vss[0:nb, 4 + g:5 + g], vnb[0:nb, 128 * g:128 * g + 128],
                    ALU.mult, ALU.mult, r=gk + ["vss2", "vnb"], w=vk_)
            cp("dve", vtt[0:nb, :], vnt[0:nb, :], r=vk_, w=tk)
            if sample:
                final_refs.append(dma("pool", o_gv[j * SL:(j + 1) * SL, :], vnt[0:nb, :], f"o_gv{j % 2}", r=vk_, w=[("o_gv", j)]))

        def vspatial(j):
            vtt, tk = vtokb[j % 2]
            for g in range(4):
                mm(ps[4 + g][:, j * nb:(j + 1) * nb], onesb[0:1, :], bsb[0:1, g, 0:nb], start=True, stop=False,
                   r=["onesb", "bsb"], w=[("ps", 4 + g)])
                mm(ps[4 + g][:, j * nb:(j + 1) * nb], vtt[0:nb, 128 * g:128 * g + 128], WmT[0:nb, g, 0:nb], start=False, stop=True,
                   r=tk + ["WmT"], w=[("ps", 4 + g)])

        vproj(0)
        for j in range(nj):
            vchain(j)
            if j + 1 < nj:
                vproj(j + 1)
            vspatial(j)
        for g in range(4):
            tt("dve", ycat[:, g, 0:N], uT[:, g, 0:N], ps[4 + g][:, 0:N], ALU.mult, r=[("G", g), ("ps", 4 + g)], w=[("G", 4 + g)])
        for c in range(4):
            s, W = wload("inab", 2 + c)
            bgc, bxb, bgb = mmbank(), mmbank(), mmbank()
            for bank, off in ((bgc, 0), (bxb, 128), (bgb, 256)):
                proj_fm(s, W, off, N, lambda k: H[:, k, 0:N], lambda k: ("H", k), bank)
            t0 = rot("tmp", 4)
            act(tmp[:, t0, 0:N], ps[bxb][:, 0:N], AF.Copy, r=[("ps", bxb)], w=[("tmp", t0)])
            z = rot("zb", 2)
            zb = zbuf[:, z, 0:nseq * (2 + L)].rearrange("p (s l) -> p s l", s=nseq)
            cp("dve", zb[:, :, 0:2], pbh[:, c, :, :], r=[pbk], w=[("zb", z)])
            tt("dve", zb[:, :, 2:2 + L], v3(ps[bgc][:, 0:N], nseq, L), v3(tmp[:, t0, 0:N], nseq, L), ALU.mult,
               r=[("ps", bgc), ("tmp", t0)], w=[("zb", z)])
            t1 = rot("tmp", 4)
            conv3(zb, ("zb", z), zb[:, :, 2:2 + L], [("zb", z)], v3(tmp[:, t1, 0:N], nseq, L), ("tmp", t1), (CB(0, c), CB(1, c), CB(2, c)), nseq, L)
            cp("dve", pbh[:, c, :, :], zb[:, :, L:L + 2], r=[("zb", z)], w=[pbk])
            tt("dve", ycat[:, 4 + c, 0:N], ps[bgb][:, 0:N], tmp[:, t1, 0:N], ALU.mult, r=[("ps", bgb), ("tmp", t1)], w=[("G", 8 + c)])
        for half in range(2):
            s, W = wload("outab", half)
            for o in range(4):
                b = mmbank()
                proj_fm(s, W, o * 128, N, lambda k: ycat[:, k, 0:N], lambda k: ("G", 4 + k), b)
                resid(half * 4 + o, b, N)

    def ffn(l, sample, N, nseq, L):
        zh = zhS if sample else zhP
        zk = ("zhS", l) if sample else ("zhP", l)
        norm(CF(l, 0), N)
        pending = None

        def product(jq, res):
            tt("dve", G[:, jq, 0:N], tmp[:, res["g"], 0:N], tmp[:, res["u"], 0:N], ALU.mult,
               r=[("tmp", res["g"]), ("tmp", res["u"])], w=[("G", jq)])

        for j in range(11):
            s, W = wload(f"up{l}", j)
            for q in range(2):
                res = {}
                chunks = (("g", q * 128, 2 * j + q), ("u", 256 + q * 128, 22 + 2 * j + q))
                zs = [rot("zb", 2), rot("zb", 2)]
                zbs = [zbuf[:, z, 0:nseq * (2 + L)].rearrange("p (s l) -> p s l", s=nseq) for z in zs]
                for (which, off, ch), z, zb in zip(chunks, zs, zbs):
                    cp("pool", zb[:, :, 0:2], zh[:, l, ch, :, :], r=[zk], w=[("zb", z)])
                for (which, off, ch), z, zb in zip(chunks, zs, zbs):
                    b = mmbank()
                    proj_fm(s, W, off, N, lambda k: H[:, k, 0:N], lambda k: ("H", k), b)
                    act(zb[:, :, 2:2 + L], v3(ps[b][:, 0:N], nseq, L), AF.Copy, r=[("ps", b)], w=[("zb", z)])
                    t0 = rot("tmp", 4)
                    conv3(zb, ("zb", z), v3(ps[b][:, 0:N], nseq, L), [("ps", b)], v3(tmp[:, t0, 0:N], nseq, L), ("tmp", t0),
                          (CW(l, 0, ch), CW(l, 1, ch), CW(l, 2, ch)), nseq, L)
                    cp("pool", zh[:, l, ch, :, :], zb[:, :, L:L + 2], r=[("zb", z)], w=[zk])
                    res[which] = t0
                act(tmp[:, res["g"], 0:N], tmp[:, res["g"], 0:N], AF.Silu, r=[("tmp", res["g"])], w=[("tmp", res["g"])])
                product(2 * j + q, res)
        for oc in range(8):
            s, W = wload(f"down{l}", oc)
            b = mmbank()
            for k in range(22):
                mm(ps[b][:, 0:N], wr(s, W, k, 0, 128), G[:, k, 0:N], start=(k == 0), stop=(k == 21), r=[("wr", s), ("G", k)], w=[("ps", b)])
            resid(oc, b, N)

    def attn(sample, ti, N):
        tok0 = 0 if sample else ti * TN
        o_k, o_v, o_lf = (o_ks, o_vs, o_lfs) if sample else (o_kp, o_vp, o_lfp)
        norm(CM(1, 0), N)
        oT = H
        tasks = [(which, slots[half], half * 4 + o, o) for which, slots in (("q", (0, 1)), ("k", (2, 3))) for half in range(2) for o in range(4)]
        st8 = {}

        def stageA(i):
            which, slot, p, o = tasks[i]
            if o == 0:
                st8["sw"] = wload("inc", slot)
            s, W = st8["sw"]
            b = mmbank()
            proj_fm(s, W, o * 128, N, lambda k: H[:, k, 0:N], lambda k: ("H", k), b)
            si = i % 2
            act(sqh[:, si, 0:N], ps[b][:, 0:N], AF.Square, r=[("ps", b)], w=[("E", si)])
            st8[i] = (b, si)

        def stageB(i):
            which, slot, p, o = tasks[i]
            b, si = st8[i]
            b2 = 4 + i % 2
            mm(ps[b2][:, 0:N], blkones[:, :], sqh[:, si, 0:N], r=[("E", si), "blkones"], w=[("ps", b2)])
            act(rs[:, 1, 0:N], ps[b2][:, 0:N], AF.Ln, scale=1.0 / 64, bias=epsc[:, 0:1], r=[("ps", b2), "epsc"], w=[("rs", 1)])
            act(rs[:, 1, 0:N], rs[:, 1, 0:N], AF.Exp, scale=-0.5, r=[("rs", 1)], w=[("rs", 1)])
            if which == "q":
                for hh in range(2):
                    rw = slice(64 * hh, 64 * hh + 64)
                    stt(Gq[rw, 2 * p + hh, 0:N], ps[b][rw, 0:N], qkcol[rw, 0:1], rs[rw, 1, 0:N], ALU.mult, ALU.mult,
                        r=[("ps", b), ("rs", 1), "qkcol"], w=[("G", 2 * p + hh)])
            else:
                t0 = rot("tmp", 4)
                stt(tmp[:, t0, 0:N], ps[b][:, 0:N], qkcol[:, 1:2], rs[:, 1, 0:N], ALU.mult, ALU.mult,
                    r=[("ps", b), ("rs", 1), "qkcol"], w=[("tmp", t0)])
                if sample:
                    cp("dve", kTs[:, p, 0:N], tmp[:, t0, 0:N], r=[("tmp", t0)], w=[("kTs", p)])
                else:
                    for hh in range(2):
                        dma("pool", kscr[2 * p + hh, :, tok0:tok0 + N], tmp[64 * hh:64 * hh + 64, t0, 0:N], f"kst{t0}_{hh}",
                            r=[("tmp", t0)], w=[("kscr", 2 * p + hh)])
                st8[("t", i)] = t0

        def stageC(i):
            which, slot, p, o = tasks[i]
            if which != "k":
                return
            t0 = st8[("t", i)]
            ko = 0
            bt = tbank()
            nblk = (N + 127) // 128
            for jb in range(nblk):
                nb = min(128, N - jb * 128)
                tr(ps[bt][0:nb, jb * 128:jb * 128 + 128], tmp[:, t0, jb * 128:jb * 128 + nb], identf, r=[("tmp", t0), "cstf"], w=[("ps", bt)])
            nb = min(128, N)
            cp("dve", kost[0:nb, ko, 0:nblk, :], ps[bt][0:nb, 0:nblk * 128].rearrange("p (j c) -> p j c", c=128),
               r=[("ps", bt)], w=[("kost", ko)])
            final_refs.append(dma("pool", o_k[tok0:tok0 + N, p * 128:(p + 1) * 128].rearrange("(j t) c -> t j c", t=nb),
                                  kost[0:nb, ko, 0:nblk, :], f"o_k{ko}", r=[("kost", ko)], w=[("o_k", ti, p)]))

        stageA(0)
        for i in range(len(tasks)):
            if i + 1 < len(tasks):
                stageA(i + 1)
            stageB(i)
            if i >= 1:
                stageC(i - 1)
        stageC(len(tasks) - 1)
        chk('a_qk')
        sv = [wload("inc", 4), wload("inc", 5)]
        nb = SL if sample else 128
        for jb in range(N // nb):
            st = rot("stg", 2)
            for half in range(2):
                s, W = sv[half]
                b = mmbank()
                for k in range(8):
                    mm(ps[b][0:nb, 0:512], H[:, k, jb * nb:(jb + 1) * nb], wr(s, W, k, 0, 512), start=(k == 0), stop=(k == 7),
                       r=[("wr", s), ("H", k)], w=[("ps", b)])
                if "vnoact" not in os.environ.get("KDBG", ""):
                    act(stg[0:nb, st, 512 * half:512 * half + 512], ps[b][0:nb, :], AF.Copy, r=[("ps", b)], w=[("stg", st)])
                if "vnodve" in os.environ.get("KDBG", ""):
                    pass
                elif sample:
                    cp("dve", Vnew[0:nb, jb, 512 * half:512 * half + 512], stg[0:nb, st, 512 * half:512 * half + 512], r=[("stg", st)], w=VK(14, 4))
                else:
                    cp("dve", V[0:nb, tok0 // 128 + jb, 512 * half:512 * half + 512], stg[0:nb, st, 512 * half:512 * half + 512],
                       r=[("stg", st)], w=[("V", tok0 // 128 + jb)])
            if "ovsp" in os.environ.get("KDBG", ""):
                final_refs.append(dma("sp", o_v[tok0 + jb * nb:tok0 + (jb + 1) * nb, :], stg[0:nb, st, :], f"stg{st}", r=[("stg", st)], w=[("o_v", ti, jb)]))
            elif "noov" in os.environ.get("KDBG", ""):
                pass
            else:
                final_refs.append(dma("pool", o_v[tok0 + jb * nb:tok0 + (jb + 1) * nb, :], stg[0:nb, st, :], f"so{st}", r=[("stg", st)], w=[("o_v", ti, jb)]))
        chk('a_v')
        s, W = wload("inc", 6)
        b = mmbank()
        for k in range(8):
            mm(ps[b][0:NH, 0:N], wr(s, W, k, 0, NH), H[:, k, 0:N], start=(k == 0), stop=(k == 7), r=[("wr", s), ("H", k)], w=[("ps", b)])
        act(nl[:, 0:N], ps[b][0:NH, 0:N], AF.Exp, scale=-1.0, bias=negb[:, 0:1], r=[("ps", b), "negb"], w=["nl"])
        act(nl[:, 0:N], nl[:, 0:N], AF.Ln, bias=onec[0:NH, 0:1], r=["nl", "onec"], w=["nl"])
        if sample:
            for q in range(NS):
                P.add("dve", lambda e, q=q: e.tensor_tensor_scan(out=nd[:, q * SL:(q + 1) * SL], data0=onesf[:, 0:SL], data1=nl[:, q * SL:(q + 1) * SL],
                                                               initial=0.0, op0=ALU.mult, op1=ALU.add), ["nl", "onesf"], ["nd"])
        else:
            P.add("dve", lambda e: e.tensor_tensor_scan(out=nd[:, 0:N], data0=onesf[:, 0:N], data1=nl[:, 0:N], initial=ndcarry[:, 0:1],
                                                      op0=ALU.mult, op1=ALU.add), ["nl", "onesf", "ndcarry"], ["nd"])
            cp("dve", ndcarry[:, :], nd[:, N - 1:N], r=["nd"], w=["ndcarry"])
        chk('a_f1')
        ts("dve", spl[:, 0, 0:N], nd[:, 0:N], -1.0, None, ALU.mult, r=["nd"], w=[("spl", 0)])
        stt(r1[:, 0:N], nd[:, 0:N], -1.0, spl[:, 0, 0:N], ALU.mult, ALU.subtract, r=["nd", ("spl", 0)], w=[("rs", 0)])
        cp("dve", spl[:, 1, 0:N], r1[:, 0:N], r=[("rs", 0)], w=[("spl", 1)])
        tt("dve", r1[:, 0:N], r1[:, 0:N], spl[:, 1, 0:N], ALU.subtract, r=[("rs", 0), ("spl", 1)], w=[("rs", 0)])
        cp("dve", spl[:, 2, 0:N], r1[:, 0:N], r=[("rs", 0)], w=[("spl", 2)])
        for h in range(NH):
            base = 0 if h % 2 else 64
            for r_ in range(3):
                dma("pool", Gq[base + r_:base + r_ + 1, h, 0:N], spl[h:h + 1, r_, 0:N], f"aug{h}_{r_}", r=[("spl", r_), ("G", h)], w=[("Gaug", h, r_)])
        chk('a_f2')
        if sample:
            for q in range(NS):
                bt = tbank()
                tr(ps[bt][0:SL, 0:NH], nl[:, q * SL:(q + 1) * SL], identf[0:NH, 0:NH], r=["nl", "cstf"], w=[("ps", bt)])
                tr(ps[bt][0:SL, NH:2 * NH], nd[:, q * SL:(q + 1) * SL], identf[0:NH, 0:NH], r=["nd", "cstf"], w=[("ps", bt)])
                ts("dve", lfst[0:SL, q, :], ps[bt][0:SL, 0:NH], -1.0, None, ALU.mult, r=[("ps", bt)], w=["lfst"])
                cp("dve", ndtoks[:, q, :], ps[bt][0:SL, NH:2 * NH], r=[("ps", bt)], w=["ndtoks"])
            final_refs.append(dma("pool", o_lf.rearrange("(q t) h -> t q h", t=SL), lfst[0:SL, 0:NS, :], "o_lf", r=["lfst"], w=[("o_lf", ti)]))
        else:
            for jb in range(N // 128):
                bt = tbank()
                tr(ps[bt][:, 0:NH], nl[:, jb * 128:(jb + 1) * 128], identf[0:NH, 0:NH], r=["nl", "cstf"], w=[("ps", bt)])
                tr(ps[bt][:, NH:2 * NH], nd[:, jb * 128:(jb + 1) * 128], identf[0:NH, 0:NH], r=["nd", "cstf"], w=[("ps", bt)])
                ts("dve", lfst[:, jb, :], ps[bt][:, 0:NH], -1.0, None, ALU.mult, r=[("ps", bt)], w=["lfst"])
                cp("dve", ndtok[:, tok0 // 128 + jb, :], ps[bt][:, NH:2 * NH], r=[("ps", bt)], w=[("ndtok", tok0 // 128 + jb)])
            final_refs.append(dma("pool", o_lf[tok0:tok0 + N, :].rearrange("(j t) h -> t j h", t=128), lfst[:, 0:N // 128, :], "o_lf",
                                  r=["lfst"], w=[("o_lf", ti)]))

        chk('a_f3')
        def head_blocks(h, keys, nq, qcol0, bo, bd):
            odd = h % 2
            p = h // 2
            rw = slice(64, 128) if odd else slice(0, 64)
            nkb = len(keys)
            qk = [("G", h)] + [("Gaug", h, r_) for r_ in range(3)]
            pend = []

            def pv(kb, eslot, c0, lv, vk, nbk):
                et, eo, ekey = eslot
                mm(ps[bo][:, c0:nq], lv, E[0:nbk, et, eo + c0:eo + nq], start=(kb == 0), stop=(kb == nkb - 1), r=[ekey] + vk, w=[("ps", bo)])
                mm(ps[bd][:, c0:nq], onesb[0:nbk, :], E[0:nbk, et, eo + c0:eo + nq], start=(kb == 0), stop=(kb == nkb - 1), r=[ekey, "onesb"], w=[("ps", bd)])

            small = nq <= 16
            la = 3 if small else 2
            for kb, (lk, kk, bias, bk, lv, vk, dg, nbk) in enumerate(keys):
                c0 = 0 if dg is None else dg
                if small:
                    bs_, so = (0, 1, 6, 7)[rot("ss", 4)], 0
                    skey = ("ps", bs_)
                    el = rot("es", 12)
                    et, eo = el % 3, 16 * (el // 3)
                    ekey = ("Es", et, eo)
                else:
                    bs_, so = (0, 1, 6)[rot("sb3", 3)], 0
                    skey = ("ps", bs_)
                    et, eo = rot("E", 3), 0
                    ekey = ("E", et)
                mm(ps[bs_][0:nbk, so + c0:so + nq], lk, Gq[:, h, qcol0 + c0:qcol0 + nq], start=True, stop=(dg is None), r=qk + kk, w=[skey])
                if dg is not None:
                    dgw = min(128, nq - c0)
                    mm(ps[bs_][0:nbk, so + c0:so + c0 + dgw], identb[0:nbk, 0:nbk], negmaskb[0:nbk, 0:dgw], start=False, stop=True,
                       r=["identb", "negmaskb"], w=[skey])
                act(E[0:nbk, et, eo + c0:eo + nq], ps[bs_][0:nbk, so + c0:so + nq], AF.Exp, bias=bias, r=[skey] + bk, w=[ekey])
                if len(pend) >= la:
                    pv(*pend.pop(0))
                pend.append((kb, (et, eo, ekey), c0, lv, vk, nbk))
            while len(pend) > 1:
                pv(*pend.pop(0))
            pv(*pend.pop())
            ri = rot("tmp", 4)
            act(tmp[rw, ri, 0:nq], ps[bd][rw, 0:nq], AF.Ln, r=[("ps", bd)], w=[("tmp", ri)])
            act(tmp[rw, ri, 0:nq], tmp[rw, ri, 0:nq], AF.Exp, scale=-1.0, r=[("tmp", ri)], w=[("tmp", ri)])
            tt("dve", oT[rw, p, qcol0:qcol0 + nq], ps[bo][rw, 0:nq], tmp[rw, ri, 0:nq], ALU.mult, r=[("ps", bo), ("tmp", ri)], w=[("H", p)])

        if not sample:
            nk = tok0 + N
            for h in range(NH):
                odd = h % 2
                p = h // 2
                rw = slice(64, 128) if odd else slice(0, 64)
                dma("sp", kbuf[odd][rw, 0:nk], kscr[h, :, 0:nk], f"kb{odd}", r=[("kscr", h)], w=[("kb", odd)])
                keys = []
                for kb in range(nk // 128):
                    j = kb - tok0 // 128
                    keys.append((kbuf[odd][:, kb * 128:(kb + 1) * 128], [("kb", odd)], ndtok[:, kb, h:h + 1], [("ndtok", kb)],
                                 V[:, kb, p * 128:(p + 1) * 128], [("V", kb)], (j * 128 if j >= 0 else None), 128))
                bo, bd = (2, 3) if h % 2 == 0 else (4, 5)
                head_blocks(h, keys, N, 0, bo, bd)
        else:
            for q in range(NS):
                dma("sp", ltok[:, 0:NPB, :], clf[q, :, :].rearrange("(k s) h -> s k h", s=128), "ltok", w=VK(13, 1))
                for g4 in range((NPB + 3) // 4):
                    bt = tbank()
                    n4 = min(4, NPB - 4 * g4)
                    for kk in range(n4):
                        tr(ps[bt][0:NH, kk * 128:(kk + 1) * 128], ltok[:, 4 * g4 + kk, :], identf, r=VK(13, 1) + ["cstf"], w=[("ps", bt)])
                    ts("dve", nlc[0:NH, 512 * g4:512 * g4 + 128 * n4], ps[bt][0:NH, 0:128 * n4], -1.0, None, ALU.mult, r=[("ps", bt)], w=VK(8, 4))
                for c0 in range(0, PAST, TN):
                    cw = min(TN, PAST - c0)
                    P.add("dve", lambda e, c0=c0, cw=cw: e.tensor_tensor_scan(out=nlc[0:NH, c0:c0 + cw], data0=onesf[:, 0:cw], data1=nlc[0:NH, c0:c0 + cw],
                                                                           initial=(0.0 if c0 == 0 else nlc[0:NH, c0 - 1:c0]), op0=ALU.mult, op1=ALU.add),
                          VK(8, 4) + ["onesf"], VK(8, 4))
                cp("dve", r1[:, 0:1], nlc[0:NH, PAST - 1:PAST], r=VK(8, 4), w=[("rs", 0)])
                ts("dve", nlc[0:NH, 0:PAST], nlc[0:NH, 0:PAST], r1[:, 0:1], None, ALU.subtract, r=VK(8, 4) + [("rs", 0)], w=VK(8, 4))
                bt = tbank()
                for kb in range(NPB):
                    tr(ps[bt][:, kb * NH:(kb + 1) * NH], nlc[0:NH, kb * 128:(kb + 1) * 128], identf[0:NH, 0:NH], r=VK(8, 4) + ["cstf"], w=[("ps", bt)])
                cp("dve", biasc[:, 0:NPB, :], ps[bt][:, 0:NPB * NH].rearrange("p (k h) -> p k h", h=NH), r=[("ps", bt)], w=VK(12, 1))
                for p in range(8):
                    ci = p % 2
                    dma("pool", kc[ci][:, 0:NPB, :], ck[q, :, p * 128:(p + 1) * 128].rearrange("(k s) c -> s k c", s=128), f"kc{ci}", w=VK(2 * ci, 2))
                    dma("pool", vc[ci][:, 0:NPB, :], cv[q, :, p * 128:(p + 1) * 128].rearrange("(k s) c -> s k c", s=128), f"vc{ci}", w=VK(4 + 2 * ci, 2))
                    for g8 in range((NPB + 7) // 8):
                        n8 = min(8, NPB - 8 * g8)
                        bt = tbank()
                        pbf = ps[bt][:, :].bitcast(BF16)
                        for kk in range(n8):
                            tr(pbf[:, kk * 128:(kk + 1) * 128], kc[ci][:, 8 * g8 + kk, :], identb[:, :], r=VK(2 * ci, 2) + ["identb"], w=[("ps", bt)])
                        cp("dve", kbuf[0][0:64, 1024 * g8:1024 * g8 + 128 * n8], pbf[0:64, 0:128 * n8], r=[("ps", bt)], w=[("kb", 0)])
                        cp("dve", kbuf[1][64:128, 1024 * g8:1024 * g8 + 128 * n8], pbf[64:128, 0:128 * n8], r=[("ps", bt)], w=[("kb", 1)])
                    cp("dve", kbuf[0][0:64, PAST:PAST + SL], kTs[0:64, p, q * SL:(q + 1) * SL], r=[("kTs", p)], w=[("kb", 0)])
                    cp("dve", kbuf[1][64:128, PAST:PAST + SL], kTs[64:128, p, q * SL:(q + 1) * SL], r=[("kTs", p)], w=[("kb", 1)])
                    for hh in range(2):
                        h = 2 * p + hh
                        keys = []
                        for kb in range(NPB):
                            keys.append((kbuf[hh][:, kb * 128:(kb + 1) * 128], [("kb", hh)], biasc[:, kb, h:h + 1], VK(12, 1),
                                         vc[ci][:, kb, :], VK(4 + 2 * ci, 2), None, 128))
                        keys.append((kbuf[hh][:, PAST:PAST + SL], [("kb", hh)], ndtoks[:, q, h:h + 1], ["ndtoks"],
                                     Vnew[0:SL, q, p * 128:(p + 1) * 128], VK(14, 4), 0, SL))
                        bo, bd = (2, 3) if hh == 0 else (4, 5)
                        head_blocks(h, keys, SL, q * SL, bo, bd)
        chk('a_core')
        for half in range(2):
            s, W = wload("outc", half)
            for o in range(4):
                b = mmbank()
                proj_fm(s, W, o * 128, N, lambda k: oT[:, k, 0:N], lambda k: ("H", k), b)
                resid(half * 4 + o, b, N)

    def store_y(dst, N, ti):
        for b in range((N + 127) // 128):
            nb = min(128, N - b * 128)
            st = rot("stg", 2)
            for g in range(2):
                bank = tbank()
                for cc in range(4):
                    c = 4 * g + cc
                    tr(ps[bank][0:nb, cc * 128:(cc + 1) * 128], xT[:, c, b * 128:b * 128 + nb], identf, r=[("xT", c), "cstf"], w=[("ps", bank)])
                act(stg[0:nb, st, 512 * g:512 * g + 512], ps[bank][0:nb, :], AF.Copy, r=[("ps", bank)], w=[("stg", st)])
            final_refs.append(dma("pool", dst[b * 128:b * 128 + nb, :], stg[0:nb, st, :], f"so{st}", r=[("stg", st)], w=[("o_y", ti, b)]))

    def chk(name):
        if stop == name:
            raise _Stop()

    def state_out(src4, nrow, dst, keys, tag):
        nch = src4.shape[1]
        for g in range((nch + 3) // 4):
            n4 = min(4, nch - 4 * g)
            bank = tbank()
            st = rot("stg", 2)
            for cc in range(n4):
                tr(ps[bank][0:nrow, cc * 128:(cc + 1) * 128], src4[:, 4 * g + cc, :, :].rearrange("p s r -> p (s r)"), identf, r=list(keys) + ["cstf"], w=[("ps", bank)])
            act(stg[0:nrow, st, 0:128 * n4], ps[bank][0:nrow, 0:128 * n4], AF.Copy, r=[("ps", bank)], w=[("stg", st)])
            final_refs.append(dma("pool", dst[:, 512 * g:512 * g + 128 * n4], stg[0:nrow, st, 0:128 * n4], f"so{st}", r=[("stg", st)], w=[(tag, g)]))

    try:
        chk("init")
        for ti in range(NT):
            load_x(xp[ti * TN:(ti + 1) * TN, :], TN)
            chk("load")
            mixer_ab(False, TN, 1, TN)
            chk("mixer")
            ffn(0, False, TN, 1, TN)
            chk("ffn0")
            attn(False, ti, TN)
            chk("attn")
            ffn(1, False, TN, 1, TN)
            store_y(o_yp[ti * TN:(ti + 1) * TN, :], TN, ti)
            chk("tile")
        load_x(xs, NSTOK)
        mixer_ab(True, NSTOK, NS, SL)
        chk("smixer")
        ffn(0, True, NSTOK, NS, SL)
        chk("sffn0")
        attn(True, NT, NSTOK)
        chk("sattn")
        ffn(1, True, NSTOK, NS, SL)
        store_y(o_ys, NSTOK, NT)
        chk("stile")
        state_out(pbhP[:, :, :, :], 2, o_cbp, ["pbhP"], "o_cbp")
        state_out(pbhS[:, :, :, :], 2 * NS, o_cbs, ["pbhS"], "o_cbs")
        for l in range(2):
            state_out(zhP[:, l, :, :, :], 2, o_ffp[l, :, :], [("zhP", l)], f"o_ffp{l}")
            state_out(zhS[:, l, :, :, :], 2 * NS, o_ffs[l, :, :], [("zhS", l)], f"o_ffs{l}")
    except _Stop:
        if "nostore" not in os.environ.get("KDBG", ""):
            store_y(o_yp[0:TN, :], TN, 99)
    P.add("sp", lambda e: None, extra=final_refs)
    P.emit(nc, es)
    es.close()
    return nc


def make_consts():
    c = np.zeros((128, 384), np.float32)
    c[:, 0:128] = np.eye(128, dtype=np.float32)
    s = np.arange(128)[:, None]
    t = np.arange(128)[None, :]
    c[:, 128:256] = np.where(s <= t, 0.0, NEGM).astype(np.float32)
    c[:, 256:384] = 1.0
    return c


def core_inputs(inp, i, NS, n_cores):
    a = lambda x: np.ascontiguousarray(np.asarray(x, dtype=np.float32))
    sl = slice(NS * i, NS * (i + 1))
    m = {
        "xp": a(inp["x_prompt"][i]), "xs": a(inp["x_sample"][sl]).reshape(NS * SL, D),
        "st_cb": a(inp["state_conv_b"][0, sl]).reshape(NS * 2, 512),
        "st_ffn": a(inp["state_ffn"][:, sl]).reshape(2, NS * 2, DUP),
        "ck": a(inp["cache_k"][0, sl]).reshape(NS, -1, D), "cv": a(inp["cache_v"][0, sl]).reshape(NS, -1, D),
        "clf": a(inp["cache_logf"][0, sl]),
        "norm_mix": a(inp["norm_mix"]), "norm_ffn": a(inp["norm_ffn"]), "w_in_ab": a(inp["w_in_ab"][0]),
        "sgu_norm": a(inp["sgu_norm"][0]), "w_spatial": a(inp["w_spatial"][0]), "b_spatial": a(inp["b_spatial"][0]),
        "conv_b": a(inp["conv_b"][0]), "w_out_ab": a(inp["w_out_ab"][0]), "w_in_c": a(inp["w_in_c"][0]),
        "b_forget": a(inp["b_forget"][0]).reshape(NH, 1), "q_norm": a(inp["q_norm"][0]).reshape(64, 1),
        "k_norm": a(inp["k_norm"][0]).reshape(64, 1), "w_out_c": a(inp["w_out_c"][0]), "w_up": a(inp["w_up"]),
        "conv_ffn": a(inp["conv_ffn"]), "w_down": a(inp["w_down"]), "cst": make_consts(),
    }
    return m


def assemble(results, B, T, NS):
    cat = lambda k: np.stack([np.asarray(r[k], dtype=np.float32) for r in results])
    DB = B * NS
    yp = cat("o_yp")
    ys = cat("o_ys").reshape(DB, SL, D)
    cbp = cat("o_cbp")[None]
    cbs = cat("o_cbs").reshape(1, DB, 2, 512)
    gvs = cat("o_gv").reshape(1, DB, SL, 512)
    kp = cat("o_kp").reshape(1, B, T, NH, 64)
    vp = cat("o_vp").reshape(1, B, T, NH, 64)
    lfp = cat("o_lfp")[None]
    ks = cat("o_ks").reshape(1, DB, SL, NH, 64)
    vs = cat("o_vs").reshape(1, DB, SL, NH, 64)
    lfs = cat("o_lfs").reshape(1, DB, SL, NH)
    ffp = np.ascontiguousarray(cat("o_ffp").transpose(1, 0, 2, 3))
    ffs = np.ascontiguousarray(cat("o_ffs").reshape(B, 2, NS, 2, DUP).transpose(1, 0, 2, 3, 4)).reshape(2, DB, 2, DUP)
    return (yp, ys, cbp, cbs, gvs, kp, vp, lfp, ks, vs, lfs, ffp, ffs)


def kernel(**inputs):
    B, T, _ = inputs["x_prompt"].shape
    DB = inputs["x_sample"].shape[0]
    NS = DB // B
    PAST = inputs["cache_k"].shape[2]
    nc = build(T, NS, PAST)
    in_maps = [core_inputs(inputs, i, NS, B) for i in range(B)]
    res = run_bass_kernel_spmd(nc, in_maps, core_ids=list(range(B)))
    return assemble(res.results, B, T, NS)
```

```python
import os
import sys
from contextlib import ExitStack

import numpy as np
import concourse.bass as bass
import concourse.mybir as mybir
from concourse.bass_utils import run_bass_kernel_spmd

F32 = mybir.dt.float32
BF16 = mybir.dt.bfloat16
AF = mybir.ActivationFunctionType
ALU = mybir.AluOpType

D = 1024
DUP = 5632
DFF = 2816
NH = 16
TN = 512
SL = 16
EPS = 1e-6
NEGM = -30000.0
GELU = AF.Gelu_apprx_tanh
ENGS = ("pe", "act", "dve", "pool", "sp")
EPOCH = 20000


class Op:
    __slots__ = ("fn", "waits", "dma", "sig", "cnt", "tag")

    def __init__(self, fn, waits, dma):
        self.fn, self.waits, self.dma, self.sig, self.cnt = fn, waits, dma, False, 0
        f = sys._getframe(2)
        tg = []
        while f is not None and len(tg) < 4:
            tg.append(str(f.f_lineno))
            f = f.f_back
        self.tag = "<".join(tg)


class Prog:
    def __init__(self):
        self.ops = {e: [] for e in ENGS}
        self.lastw, self.readers, self.dcount = {}, {}, {}

    def _need(self, eng, dma, ref, raw):
        if ref[0] == "d" or ref[1] != eng or dma is not None:
            return True
        if eng == "pe":
            return False
        return True

    def add(self, eng, fn, r=(), w=(), dma=None, extra=()):
        idx = len(self.ops[eng])
        if dma is not None:
            c = self.dcount.get(dma, 0) + 1
            self.dcount[dma] = c
            me = ("d", dma, c)
        else:
            me = ("e", eng, idx)
        waits = set(extra)
        for b in r:
            lw = self.lastw.get(b)
            if lw is not None and self._need(eng, dma, lw, True):
                waits.add(lw)
        for b in w:
            lw = self.lastw.get(b)
            if lw is not None and self._need(eng, dma, lw, False):
                waits.add(lw)
            for rr in self.readers.get(b, {}).values():
                if rr != me and self._need(eng, dma, rr, False):
                    waits.add(rr)
        for b in w:
            self.lastw[b] = me
            self.readers[b] = {}
        for b in r:
            self.readers.setdefault(b, {})[me[:2]] = me
        self.ops[eng].append(Op(fn, waits, dma))
        return me

    def emit(self, nc, es):
        for e in ENGS:
            for op in self.ops[e]:
                for ref in op.waits:
                    if ref[0] == "e":
                        self.ops[ref[1]][ref[2]].sig = True
        esem = {}
        for e in ENGS:
            c = 0
            for op in self.ops[e]:
                if op.sig:
                    c += 1
                    op.cnt = c
            esem[e] = [es.enter_context(nc.semaphore(f"s_{e}_{k}")) for k in range(c // EPOCH + 1)]
        dsem = {k: es.enter_context(nc.semaphore(f"d_{k}")) for k in self.dcount}
        block = es.enter_context(nc.Block())
        engobj = {"pe": block.tensor, "act": block.scalar, "dve": block.vector, "pool": block.gpsimd, "sp": block.sync}

        def body_for(ename):
            def body(e):
                waited = {}
                for op in self.ops[ename]:
                    for ref in sorted(op.waits, key=str):
                        if ref[0] == "e":
                            c = self.ops[ref[1]][ref[2]].cnt
                            k = ("e", ref[1])
                            if waited.get(k, 0) >= c:
                                continue
                            waited[k] = c
                            ep = (c - 1) // EPOCH
                            e.wait_ge(esem[ref[1]][ep], c - ep * EPOCH)
                        else:
                            k = ("d", ref[1])
                            if waited.get(k, 0) >= ref[2]:
                                continue
                            waited[k] = ref[2]
                            e.wait_ge(dsem[ref[1]], 16 * ref[2])
                    try:
                        ins = op.fn(e)
                    except Exception as ex:
                        raise RuntimeError(f"emit failed for op at line {op.tag} on {ename}: {str(ex)[:300]}") from None
                    if ins is None:
                        continue
                    if op.dma is not None:
                        ins.then_inc(dsem[op.dma], 16)
                    elif op.sig:
                        ep = (op.cnt - 1) // EPOCH
                        ins.then_inc(esem[ename][ep], 1)
            return body

        for ename in ENGS:
            engobj[ename](body_for(ename))


class _Stop(Exception):
    pass


def build(T, NS, PAST, stop=None):
    NT = T // TN
    NBLK = T // 128
    NSTOK = NS * SL
    NPB = PAST // 128
    VBLK = max(NBLK, 18)
    KW = max(T, PAST + SL)
    nc = bass.Bass("TRN2", target_bir_lowering=False)
    P = Prog()
    es = ExitStack()

    def din(name, shape):
        return nc.dram_tensor(name, list(shape), F32, kind="ExternalInput").ap()

    def dout(name, shape):
        return nc.dram_tensor(name, list(shape), F32, kind="ExternalOutput").ap()

    def dscr(name, shape):
        return nc.dram_tensor(name, list(shape), BF16, kind="Internal").ap()

    xp = din("xp", [T, D]); xs = din("xs", [NSTOK, D])
    st_cb = din("st_cb", [NS * 2, 512]); st_ffn = din("st_ffn", [2, NS * 2, DUP])
    ck = din("ck", [NS, PAST, D]); cv = din("cv", [NS, PAST, D]); clf = din("clf", [NS, PAST, NH])
    norm_mix = din("norm_mix", [2, D]); norm_ffn = din("norm_ffn", [2, D])
    w_in_ab = din("w_in_ab", [D, 2560]); sgu_norm = din("sgu_norm", [4, 128])
    w_spatial = din("w_spatial", [4, 128, 128]); b_spatial = din("b_spatial", [4, 128])
    conv_b = din("conv_b", [3, 512]); w_out_ab = din("w_out_ab", [D, D])
    w_in_c = din("w_in_c", [D, 3088]); b_forget = din("b_forget", [NH, 1])
    q_norm = din("q_norm", [64, 1]); k_norm = din("k_norm", [64, 1]); w_out_c = din("w_out_c", [D, D])
    w_up = din("w_up", [2, D, DUP]); conv_ffn = din("conv_ffn", [2, 3, DUP]); w_down = din("w_down", [2, DFF, D])
    cst = din("cst", [128, 384])

    o_yp = dout("o_yp", [T, D]); o_ys = dout("o_ys", [NSTOK, D])
    o_cbp = dout("o_cbp", [2, 512]); o_cbs = dout("o_cbs", [NS * 2, 512]); o_gv = dout("o_gv", [NSTOK, 512])
    o_kp = dout("o_kp", [T, D]); o_vp = dout("o_vp", [T, D]); o_lfp = dout("o_lfp", [T, NH])
    o_ks = dout("o_ks", [NSTOK, D]); o_vs = dout("o_vs", [NSTOK, D]); o_lfs = dout("o_lfs", [NSTOK, NH])
    o_ffp = dout("o_ffp", [2, 2, DUP]); o_ffs = dout("o_ffs", [2, NS * 2, DUP])
    kscr = dscr("kscr", [NH, 64, T])

    def sb(name, shape, dt=F32):
        return es.enter_context(nc.sbuf_tensor(name, list(shape), dt))

    cstf = sb("cstf", [128, 384])
    identf = cstf[:, 0:128]; negmask = cstf[:, 128:256]; negmaskb = cstf[:, 256:320].bitcast(BF16)
    identb = sb("identb", [128, 128], BF16); onesb = sb("onesb", [128, 128], BF16); blkones = sb("blkones", [128, 128], BF16)
    col = sb("col", [128, 320]); qkcol = sb("qkcol", [128, 2]); negb = sb("negb", [NH, 1])
    epsc = sb("epsc", [128, 1]); onec = sb("onec", [128, 1]); onesf = sb("onesf", [NH, TN])
    stg = sb("stg", [128, 2, 1024])
    xT = sb("xT", [128, 8, TN]); H = sb("H", [128, 8, TN], BF16); G = sb("G", [128, 22, TN], BF16)
    V = sb("V", [128, VBLK, 1024], BF16)
    kbuf = [sb("kbufE", [128, KW], BF16), sb("kbufO", [128, KW], BF16)]
    wring = sb("wring", [128, 3, 4096], BF16)
    zbuf = sb("zbuf", [128, 2, 2 + TN]); tmp = sb("tmp", [128, 4, TN]); rs = sb("rs", [128, 2, TN])
    pbhP = sb("pbhP", [128, 4, 1, 2]); pbhS = sb("pbhS", [128, 4, NS, 2])
    zhP = sb("zhP", [128, 2, 44, 1, 2]); zhS = sb("zhS", [128, 2, 44, NS, 2])
    E = sb("E", [128, 3, TN], BF16)
    gv = sb("gv", [128, 1, 512]); vnf = sb("vnf", [128, 512]); vtok = sb("vtok", [128, 1, 512], BF16)
    vnb = sb("vnb", [128, 512]); vss = sb("vss", [128, 8])
    WmT = sb("WmT", [128, 4, 128], BF16); bsb = sb("bsb", [1, 4, 128], BF16)
    ndtok = sb("ndtok", [128, NBLK, NH]); ndtoks = sb("ndtoks", [SL, NS, NH])
    nl = sb("nl", [NH, TN]); nd = sb("nd", [NH, TN]); ndcarry = sb("ndcarry", [NH, 1]); r1 = rs[0:NH, 0, :]
    spl = sb("spl", [NH, 3, TN], BF16)
    lfst = sb("lfst", [128, 4, NH]); kost = sb("kost", [128, 1, 4, 128]); kst = None
    sqh = E; kTs = sb("kTs", [128, 8, 64], BF16)

    ps = [es.enter_context(nc.psum_tensor(f"ps{i}", [128, 512], F32)) for i in range(8)]
    Gq = G
    sq = G[:, 14:22, :]

    def vview(b0, nb, dt, pat=None, **kw):
        a = V[:, b0:b0 + nb, :].rearrange("p b c -> p (b c)")
        if dt == F32:
            a = a.bitcast(F32)
        return a if pat is None else a.rearrange(pat, **kw)

    kc = [vview(0, 2, BF16, "p (k c) -> p k c", c=128), vview(2, 2, BF16, "p (k c) -> p k c", c=128)]
    vc = [vview(4, 2, BF16, "p (k c) -> p k c", c=128), vview(6, 2, BF16, "p (k c) -> p k c", c=128)]
    nlc = vview(8, 4, F32)
    biasc = vview(12, 1, F32, "p (k h) -> p k h", h=NH)
    ltok = vview(13, 1, F32, "p (k h) -> p k h", h=NH)
    Vnew = vview(14, 4, BF16, "p (s c) -> p s c", c=1024)
    VK = lambda b0, nb: [("V", b) for b in range(b0, b0 + nb)]

    def mm(out, lhsT, rhs, start=True, stop=True, r=(), w=()):
        return P.add("pe", lambda e: e.matmul(out, lhsT=lhsT, rhs=rhs, start=start, stop=stop), r, w)

    def tr(out, in_, ident, r=(), w=()):
        return P.add("pe", lambda e: e.transpose(out=out, in_=in_, identity=ident), r, w)

    def act(out, in_, func, r=(), w=(), **kw):
        return P.add("act", lambda e: e.activation(out=out, in_=in_, func=func, **kw), r, w)

    def tt(eng, out, in0, in1, op, r=(), w=()):
        return P.add(eng, lambda e: e.tensor_tensor(out=out, in0=in0, in1=in1, op=op), r, w)

    def ts(eng, out, in0, s1, s2, op0, op1=ALU.bypass, r=(), w=()):
        return P.add(eng, lambda e: e.tensor_scalar(out=out, in0=in0, scalar1=s1, scalar2=s2, op0=op0, op1=op1), r, w)

    def stt(out, in0, scalar, in1, op0, op1, r=(), w=()):
        return P.add("dve", lambda e: e.scalar_tensor_tensor(out=out, in0=in0, scalar=scalar, in1=in1, op0=op0, op1=op1), r, w)

    def cp(eng, out, in_, r=(), w=()):
        return P.add(eng, lambda e: e.tensor_copy(out=out, in_=in_), r, w)

    def recip(out, in_, r=(), w=()):
        return P.add("dve", lambda e: e.reciprocal(out=out, in_=in_), r, w)

    def mset(eng, ap, val, w=()):
        return P.add(eng, lambda e: e.memset(ap, val), (), w)

    def dma(q, out, in_, key, r=(), w=(), **kw):
        return P.add(q, lambda e: e.dma_start(out=out, in_=in_, **kw), r, w, dma=key)

    ctr = {"mm": 0, "tb": 0, "stg": 0, "wr": 0, "zb": 0, "tmp": 0, "E": 0, "sb": 0, "ko": 0, "sb3": 0, "ss": 0, "es": 0}

    def rot(name, n, base=0):
        v = ctr[name]
        ctr[name] = (v + 1) % n
        return base + v

    mmbank = lambda: rot("mm", 4)
    tbank = lambda: rot("tb", 2, 6)
    final_refs = []

    def v3(ap, nseq, L):
        return ap if nseq == 1 and False else ap.rearrange("p (s l) -> p s l", s=nseq)

    groups = {}

    def defgroup(name, src, KC, slots):
        wmax = max(sum(c1 - c0 for c0, c1 in s) for s in slots)
        scr = dscr("w_" + name, [len(slots), 128, KC * wmax])
        groups[name] = (scr, KC, slots, wmax)
        srcv = src.rearrange("(k p) n -> p k n", p=128)
        last = None
        for si, segs in enumerate(slots):
            wtot = sum(c1 - c0 for c0, c1 in segs)
            o = 0
            for c0, c1 in segs:
                dst = scr[si, :, 0:KC * wtot].rearrange("p (k w) -> p k w", k=KC)[:, :, o:o + c1 - c0]
                last = dma("pool", dst, srcv[:, :, c0:c1], "cast_" + name, w=[("scr", name, si, o)])
                o += c1 - c0
        for si, segs in enumerate(slots):
            o = 0
            for c0, c1 in segs:
                P.lastw[("scr", name, si, o)] = last
                o += c1 - c0

    def wload(name, si):
        scr, KC, slots, wmax = groups[name]
        segs = slots[si]
        wtot = sum(c1 - c0 for c0, c1 in segs)
        s = rot("wr", 3)
        keys, o = [], 0
        for c0, c1 in segs:
            keys.append(("scr", name, si, o)); o += c1 - c0
        dma("sp", wring[:, s, 0:KC * wtot], scr[si, :, 0:KC * wtot], f"wr{s}", r=keys, w=[("wr", s)])
        return s, wtot

    def wr(s, W, k, c0, m):
        return wring[:, s, k * W + c0:k * W + c0 + m]

    rowsrc = [(norm_mix.rearrange("l (c p) -> (l c) p", p=128), 16), (norm_ffn.rearrange("l (c p) -> (l c) p", p=128), 16),
              (conv_b.rearrange("r (c p) -> (r c) p", p=128), 12), (conv_ffn.rearrange("l r (c p) -> (l r c) p", p=128), 264)]
    CM = lambda l, c: l * 8 + c
    CF = lambda l, c: 16 + l * 8 + c
    CB = lambda r_, c: 32 + r_ * 4 + c
    CW = lambda l, r_, ch: 44 + (l * 3 + r_) * 44 + ch
    rowt = tmp[:, 1:4, 0:128]
    ikeys = []

    def idma(out, in_, key):
        ikeys.append(key)
        dma("sp", out, in_, "init", w=[key])

    idma(cstf[:, :], cst[:, :], "cstf")
    row0 = 0
    for src, n in rowsrc:
        done = 0
        while done < n:
            t_i, off = divmod(row0 + done, 128)
            m = min(n - done, 128 - off)
            idma(rowt[off:off + m, t_i, :], src[done:done + m, :], ("rowt", t_i, off))
            done += m
        row0 += n
    for j, srcn in enumerate((q_norm, k_norm)):
        for h2 in range(2):
            idma(qkcol[64 * h2:64 * h2 + 64, j:j + 1], srcn[:, :], ("qkcol", j, h2))
    idma(negb[:, :], b_forget[:, :], "negb")
    idma(stg[0:1, 0, 0:512], sgu_norm.rearrange("g c -> (g c)").rearrange("(o n) -> o n", o=1), ("stg", 0))
    idma(stg[0:1, 0, 512:1024], b_spatial.rearrange("g c -> (g c)").rearrange("(o n) -> o n", o=1), ("stgi", 0))
    for g in range(4):
        idma(stg[:, 1, 128 * g:128 * g + 128], w_spatial[g, :, :], ("stgi", 1, g))
    idma(stg[0:2 * NS, 1, 512:1024], st_cb[:, :], ("stg", 1))
    for k_ in ikeys:
        P.lastw[k_] = ("d", "init", P.dcount["init"])
    INIT = [("stg", 0), ("stg", 1)]

    mset("pool", G[:, :, :], 0.0, w=[("G", c) for c in range(22)])
    mset("pool", kbuf[0][64:128, :], 0.0, w=[("kb", 0)])
    mset("pool", kbuf[0][64:67, :], 1.0, w=[("kb", 0)])
    mset("pool", kbuf[1][0:64, :], 0.0, w=[("kb", 1)])
    mset("pool", kbuf[1][0:3, :], 1.0, w=[("kb", 1)])
    mset("pool", onesb[:, :], 1.0, w=["onesb"])
    mset("pool", blkones[:, :], 0.0, w=["blkones"])
    mset("pool", blkones[0:64, 0:64], 1.0, w=["blkones"])
    mset("pool", blkones[64:128, 64:128], 1.0, w=["blkones"])
    mset("pool", epsc[:, :], EPS, w=["epsc"])
    mset("pool", onec[:, :], 1.0, w=["onec"])
    mset("pool", onesf[:, :], 1.0, w=["onesf"])
    mset("pool", ndcarry[:, :], 0.0, w=["ndcarry"])
    mset("pool", pbhP[:, :, :, :], 0.0, w=["pbhP"])
    mset("pool", zhP[:, :, :, :, :], 0.0, w=[("zhP", l) for l in range(2)])
    mset("pool", tmp[0:1, 0, 0:128], 1.0, w=[("tmp", 0)])
    cp("dve", identb[:, :], identf, r=["cstf"], w=["identb"])
    cp("dve", negmaskb, negmask, r=["cstf"], w=["negmaskb"])

    defgroup("inab", w_in_ab, 8, [[(0, 512)], [(512, 1024)]] +
             [[(1536 + 128 * c, 1664 + 128 * c), (2048 + 128 * c, 2176 + 128 * c), (1024 + 128 * c, 1152 + 128 * c)] for c in range(4)])
    defgroup("outab", w_out_ab, 8, [[(0, 512)], [(512, 1024)]])
    defgroup("up0", w_up[0], 8, [[(256 * j, 256 * j + 256), (DFF + 256 * j, DFF + 256 * j + 256)] for j in range(11)])
    defgroup("down0", w_down[0], 22, [[(128 * o, 128 * o + 128)] for o in range(8)])
    defgroup("inc", w_in_c, 8, [[(512 * j, 512 * j + 512)] for j in range(6)] + [[(3072, 3088)]])
    defgroup("outc", w_out_c, 8, [[(0, 512)], [(512, 1024)]])
    defgroup("up1", w_up[1], 8, [[(256 * j, 256 * j + 256), (DFF + 256 * j, DFF + 256 * j + 256)] for j in range(11)])
    defgroup("down1", w_down[1], 22, [[(128 * o, 128 * o + 128)] for o in range(8)])

    for t_i in range(3):
        nr = min(128, row0 - 128 * t_i)
        rk = [k_ for k_ in ikeys if isinstance(k_, tuple) and k_[0] == "rowt" and k_[1] == t_i]
        tr(ps[0][:, 0:nr], rowt[0:nr, t_i, :], identf[0:nr, 0:nr], r=rk + ["cstf", ("tmp", t_i + 1)], w=[("ps", 0)])
        cp("dve", col[:, 128 * t_i:128 * t_i + nr], ps[0][:, 0:nr], r=[("ps", 0)], w=["col"])
    ts("dve", qkcol[:, 0:1], qkcol[:, 0:1], 0.125, None, ALU.mult, r=[("qkcol", 0, 0), ("qkcol", 0, 1), ("qkcol", 1, 0), ("qkcol", 1, 1)], w=["qkcol"])
    ts("dve", negb[:, :], negb[:, :], -1.0, None, ALU.mult, r=["negb"], w=["negb"])
    mm(ps[1][:, 0:512], tmp[0:1, 0, 0:128], stg[0:1, 0, 0:512], r=[("stg", 0), ("tmp", 0)], w=[("ps", 1)])
    cp("dve", vnb[:, :], ps[1][:, 0:512], r=[("ps", 1)], w=["vnb"])
    cp("dve", bsb[0:1, :, :], stg[0:1, 0, 512:1024].rearrange("p (g c) -> p g c", g=4), r=[("stgi", 0), ("stg", 0)], w=["bsb"])
    for g in range(4):
        tr(ps[2][:, 128 * g:128 * g + 128], stg[:, 1, 128 * g:128 * g + 128], identf, r=[("stgi", 1, g), ("stg", 1), "cstf"], w=[("ps", 2)])
    cp("dve", WmT[:, :, :], ps[2][:, :].rearrange("p (g c) -> p g c", g=4), r=[("ps", 2)], w=["WmT"])
    mset("dve", WmT[64:128, :, 0:64], 0.0, w=["WmT"])
    for cc in range(4):
        tr(ps[3][:, cc * 2 * NS:(cc + 1) * 2 * NS], stg[0:2 * NS, 1, 512 + 128 * cc:512 + 128 * cc + 128], identf[0:2 * NS, 0:2 * NS],
           r=[("stg", 1), "cstf"], w=[("ps", 3)])
    cp("dve", pbhS[:, :, :, :].rearrange("p c s r -> p c (s r)"), ps[3][:, 0:8 * NS].rearrange("p (c x) -> p c x", c=4),
       r=[("ps", 3)], w=["pbhS"])
    for l in range(2):
        for g in range(11):
            st = rot("stg", 2)
            b = mmbank()
            dma("sp", stg[0:2 * NS, st, 0:512], st_ffn[l, :, 512 * g:512 * g + 512], f"stg{st}", w=[("stg", st)])
            for cc in range(4):
                tr(ps[b][:, cc * 2 * NS:(cc + 1) * 2 * NS], stg[0:2 * NS, st, 128 * cc:128 * cc + 128], identf[0:2 * NS, 0:2 * NS],
                   r=[("stg", st), "cstf"], w=[("ps", b)])
            cp("dve", zhS[:, l, 4 * g:4 * g + 4, :, :].rearrange("p c s r -> p c (s r)"),
               ps[b][:, 0:8 * NS].rearrange("p (c x) -> p c x", c=4), r=[("ps", b)], w=[("zhS", l)])

    def load_x(src, N):
        for b in range((N + 127) // 128):
            nb = min(128, N - b * 128)
            st = rot("stg", 2)
            dma("sp", stg[0:nb, st, :], src[b * 128:b * 128 + nb, :], f"stg{st}", w=[("stg", st)])
            for g in range(2):
                bank = tbank()
                for cc in range(4):
                    c = 4 * g + cc
                    tr(ps[bank][:, cc * 128:cc * 128 + nb], stg[0:nb, st, c * 128:(c + 1) * 128], identf[0:nb, 0:nb],
                       r=[("stg", st), "cstf"], w=[("ps", bank)])
                act(xT[:, 4 * g:4 * g + 4, b * 128:b * 128 + nb], ps[bank][:, :].rearrange("p (c t) -> p c t", c=4)[:, :, 0:nb], AF.Copy,
                    r=[("ps", bank)], w=[("xT", 4 * g + cc) for cc in range(4)])

    def norm(gc0, N):
        for c in range(8):
            act(sq[:, c, 0:N], xT[:, c, 0:N], AF.Square, r=[("xT", c)], w=[("G", 14 + c)])
            mm(ps[5][:, 0:N], onesb[:, :], sq[:, c, 0:N], start=(c == 0), stop=(c == 7), r=[("G", 14 + c), "onesb"], w=[("ps", 5)])
        act(rs[:, 0, 0:N], ps[5][:, 0:N], AF.Ln, scale=1.0 / D, bias=epsc[:, 0:1], r=[("ps", 5), "epsc"], w=[("rs", 0)])
        act(rs[:, 0, 0:N], rs[:, 0, 0:N], AF.Exp, scale=-0.5, r=[("rs", 0)], w=[("rs", 0)])
        for c in range(8):
            stt(H[:, c, 0:N], xT[:, c, 0:N], col[:, gc0 + c:gc0 + c + 1], rs[:, 0, 0:N], ALU.mult, ALU.mult,
                r=[("xT", c), ("rs", 0), "col"], w=[("H", c)])

    def proj_fm(s, W, c0, N, rhs_of, rkeys, bank):
        for k in range(8):
            mm(ps[bank][:, 0:N], wr(s, W, k, c0, 128), rhs_of(k), start=(k == 0), stop=(k == 7),
               r=[("wr", s), rkeys(k)], w=[("ps", bank)])

    def resid(oc, bank, N):
        tt("dve", xT[:, oc, 0:N], xT[:, oc, 0:N], ps[bank][:, 0:N], ALU.add, r=[("xT", oc), ("ps", bank)], w=[("xT", oc)])

    def conv3(zb, zkey, cur, curkeys, A, akey, l_cols, nseq, L):
        w0, w1, w2 = l_cols
        act(A, cur, AF.Copy, scale=col[:, w2:w2 + 1], r=list(curkeys) + ["col"], w=[akey])
        stt(A, zb[:, :, 1:1 + L], col[:, w1:w1 + 1], A, ALU.mult, ALU.add, r=[zkey, akey, "col"], w=[akey])
        stt(A, zb[:, :, 0:L], col[:, w0:w0 + 1], A, ALU.mult, ALU.add, r=[zkey, akey, "col"], w=[akey])

    def mixer_ab(sample, N, nseq, L):
        pbh = pbhS if sample else pbhP
        pbk = "pbhS" if sample else "pbhP"
        norm(CM(0, 0), N)
        uT = G[:, 0:4, :]
        ycat = G[:, 4:12, :]
        s, W = wload("inab", 0)
        for oc in range(4):
            b = mmbank()
            proj_fm(s, W, oc * 128, N, lambda k: H[:, k, 0:N], lambda k: ("H", k), b)
            act(uT[:, oc, 0:N], ps[b][:, 0:N], GELU, r=[("ps", b)], w=[("G", oc)])
        s, W = wload("inab", 1)
        nb = SL if sample else 128

        def gview(c0, nch, dt):
            a = G[:, c0:c0 + nch, :].rearrange("p c t -> p (c t)")
            return a.bitcast(F32) if dt == F32 else a

        gvb = [(gv[:, 0, :], [("gv", 0)]), (gview(12, 2, F32), [("G", 12), ("G", 13)])]
        vnfb = [(vnf, ["vnf"]), (gview(14, 2, F32), [("G", 14), ("G", 15)])]
        vtokb = [(vtok[:, 0, :], [("vtok", 0)]), (gview(16, 1, BF16), [("G", 16)])]
        nj = N // nb
        pbank = {}

        def vproj(j):
            b = mmbank()
            pbank[j] = b
            for k in range(8):
                mm(ps[b][0:nb, 0:512], H[:, k, j * nb:(j + 1) * nb], wr(s, W, k, 0, 512), start=(k == 0), stop=(k == 7),
                   r=[("wr", s), ("H", k)], w=[("ps", b)])

        def vchain(j):
            b = pbank[j]
            gvt, gk = gvb[j % 2]
            vnt, vk_ = vnfb[j % 2]
            vtt, tk = vtokb[j % 2]
            vs = vss[:, 0:8] if j % 2 == 0 else None
            act(gvt[0:nb, :], ps[b][0:nb, :], GELU, r=[("ps", b)], w=gk)
            for g in range(4):
                act(vnt[0:nb, 128 * g:128 * g + 128], gvt[0:nb, 128 * g:128 * g + 128], AF.Square, accum_out=vss[0:nb, g:g + 1],
                    r=gk, w=vk_ + ["vss"])
            act(vss[0:nb, 4:8], vss[0:nb, 0:4], AF.Sqrt, scale=1.0 / 128, bias=epsc[0:nb, 0:1], r=["vss", "epsc"], w=["vss2"])
            recip(vss[0:nb, 4:8], vss[0:nb, 4:8], r=["vss2"], w=["vss2"])
            for g in range(4):
                stt(vnt[0:nb, 128 * g:128 * g + 128], gvt[0:nb, 128 * g:128 * g + 128], vss[0:nb, 4 + g:5 + g], vnb[0:nb, 128 * g:128 * g + 128],
                    ALU.mult, ALU.mult, r=gk + ["vss2", "vnb"], w=vk_)
            cp("dve", vtt[0:nb, :], vnt[0:nb, :], r=vk_, w=tk)
            if sample:
                final_refs.append(dma("pool", o_gv[j * SL:(j + 1) * SL, :], vnt[0:nb, :], f"o_gv{j % 2}", r=vk_, w=[("o_gv", j)]))

        def vspatial(j):
            vtt, tk = vtokb[j % 2]
            for g in range(4):
                mm(ps[4 + g][:, j * nb:(j + 1) * nb], onesb[0:1, :], bsb[0:1, g, 0:nb], start=True, stop=False,
                   r=["onesb", "bsb"], w=[("ps", 4 + g)])
                mm(ps[4 + g][:, j * nb:(j + 1) * nb], vtt[0:nb, 128 * g:128 * g + 128], WmT[0:nb, g, 0:nb], start=False, stop=True,
                   r=tk + ["WmT"], w=[("ps", 4 + g)])

        vproj(0)
        for j in range(nj):
            vchain(j)
            if j + 1 < nj:
                vproj(j + 1)
            vspatial(j)
        for g in range(4):
            tt("dve", ycat[:, g, 0:N], uT[:, g, 0:N], ps[4 + g][:, 0:N], ALU.mult, r=[("G", g), ("ps", 4 + g)], w=[("G", 4 + g)])
        for c in range(4):
            s, W = wload("inab", 2 + c)
            bgc, bxb, bgb = mmbank(), mmbank(), mmbank()
            for bank, off in ((bgc, 0), (bxb, 128), (bgb, 256)):
                proj_fm(s, W, off, N, lambda k: H[:, k, 0:N], lambda k: ("H", k), bank)
            t0 = rot("tmp", 4)
            act(tmp[:, t0, 0:N], ps[bxb][:, 0:N], AF.Copy, r=[("ps", bxb)], w=[("tmp", t0)])
            z = rot("zb", 2)
            zb = zbuf[:, z, 0:nseq * (2 + L)].rearrange("p (s l) -> p s l", s=nseq)
            cp("dve", zb[:, :, 0:2], pbh[:, c, :, :], r=[pbk], w=[("zb", z)])
            tt("dve", zb[:, :, 2:2 + L], v3(ps[bgc][:, 0:N], nseq, L), v3(tmp[:, t0, 0:N], nseq, L), ALU.mult,
               r=[("ps", bgc), ("tmp", t0)], w=[("zb", z)])
            t1 = rot("tmp", 4)
            conv3(zb, ("zb", z), zb[:, :, 2:2 + L], [("zb", z)], v3(tmp[:, t1, 0:N], nseq, L), ("tmp", t1), (CB(0, c), CB(1, c), CB(2, c)), nseq, L)
            cp("dve", pbh[:, c, :, :], zb[:, :, L:L + 2], r=[("zb", z)], w=[pbk])
            tt("dve", ycat[:, 4 + c, 0:N], ps[bgb][:, 0:N], tmp[:, t1, 0:N], ALU.mult, r=[("ps", bgb), ("tmp", t1)], w=[("G", 8 + c)])
        for half in range(2):
            s, W = wload("outab", half)
            for o in range(4):
                b = mmbank()
                proj_fm(s, W, o * 128, N, lambda k: ycat[:, k, 0:N], lambda k: ("G", 4 + k), b)
                resid(half * 4 + o, b, N)

    def ffn(l, sample, N, nseq, L):
        zh = zhS if sample else zhP
        zk = ("zhS", l) if sample else ("zhP", l)
        norm(CF(l, 0), N)
        pending = None

        def product(jq, res):
            tt("dve", G[:, jq, 0:N], tmp[:, res["g"], 0:N], tmp[:, res["u"], 0:N], ALU.mult,
               r=[("tmp", res["g"]), ("tmp", res["u"])], w=[("G", jq)])

        for j in range(11):
            s, W = wload(f"up{l}", j)
            for q in range(2):
                res = {}
                chunks = (("g", q * 128, 2 * j + q), ("u", 256 + q * 128, 22 + 2 * j + q))
                zs = [rot("zb", 2), rot("zb", 2)]
                zbs = [zbuf[:, z, 0:nseq * (2 + L)].rearrange("p (s l) -> p s l", s=nseq) for z in zs]
                for (which, off, ch), z, zb in zip(chunks, zs, zbs):
                    cp("pool", zb[:, :, 0:2], zh[:, l, ch, :, :], r=[zk], w=[("zb", z)])
                for (which, off, ch), z, zb in zip(chunks, zs, zbs):
                    b = mmbank()
                    proj_fm(s, W, off, N, lambda k: H[:, k, 0:N], lambda k: ("H", k), b)
                    act(zb[:, :, 2:2 + L], v3(ps[b][:, 0:N], nseq, L), AF.Copy, r=[("ps", b)], w=[("zb", z)])
                    t0 = rot("tmp", 4)
                    conv3(zb, ("zb", z), v3(ps[b][:, 0:N], nseq, L), [("ps", b)], v3(tmp[:, t0, 0:N], nseq, L), ("tmp", t0),
                          (CW(l, 0, ch), CW(l, 1, ch), CW(l, 2, ch)), nseq, L)
                    cp("pool", zh[:, l, ch, :, :], zb[:, :, L:L + 2], r=[("zb", z)], w=[zk])
                    res[which] = t0
                act(tmp[:, res["g"], 0:N], tmp[:, res["g"], 0:N], AF.Silu, r=[("tmp", res["g"])], w=[("tmp", res["g"])])
                product(2 * j + q, res)
        for oc in range(8):
            s, W = wload(f"down{l}", oc)
            b = mmbank()
            for k in range(22):
                mm(ps[b][:, 0:N], wr(s, W, k, 0, 128), G[:, k, 0:N], start=(k == 0), stop=(k == 21), r=[("wr", s), ("G", k)], w=[("ps", b)])
            resid(oc, b, N)

    def attn(sample, ti, N):
        tok0 = 0 if sample else ti * TN
        o_k, o_v, o_lf = (o_ks, o_vs, o_lfs) if sample else (o_kp, o_vp, o_lfp)
        norm(CM(1, 0), N)
        oT = H
        tasks = [(which, slots[half], half * 4 + o, o) for which, slots in (("q", (0, 1)), ("k", (2, 3))) for half in range(2) for o in range(4)]
        st8 = {}

        def stageA(i):
            which, slot, p, o = tasks[i]
            if o == 0:
                st8["sw"] = wload("inc", slot)
            s, W = st8["sw"]
            b = mmbank()
            proj_fm(s, W, o * 128, N, lambda k: H[:, k, 0:N], lambda k: ("H", k), b)
            si = i % 2
            act(sqh[:, si, 0:N], ps[b][:, 0:N], AF.Square, r=[("ps", b)], w=[("E", si)])
            st8[i] = (b, si)

        def stageB(i):
            which, slot, p, o = tasks[i]
            b, si = st8[i]
            b2 = 4 + i % 2
            mm(ps[b2][:, 0:N], blkones[:, :], sqh[:, si, 0:N], r=[("E", si), "blkones"], w=[("ps", b2)])
            act(rs[:, 1, 0:N], ps[b2][:, 0:N], AF.Ln, scale=1.0 / 64, bias=epsc[:, 0:1], r=[("ps", b2), "epsc"], w=[("rs", 1)])
            act(rs[:, 1, 0:N], rs[:, 1, 0:N], AF.Exp, scale=-0.5, r=[("rs", 1)], w=[("rs", 1)])
            if which == "q":
                for hh in range(2):
                    rw = slice(64 * hh, 64 * hh + 64)
                    stt(Gq[rw, 2 * p + hh, 0:N], ps[b][rw, 0:N], qkcol[rw, 0:1], rs[rw, 1, 0:N], ALU.mult, ALU.mult,
                        r=[("ps", b), ("rs", 1), "qkcol"], w=[("G", 2 * p + hh)])
            else:
                t0 = rot("tmp", 4)
                stt(tmp[:, t0, 0:N], ps[b][:, 0:N], qkcol[:, 1:2], rs[:, 1, 0:N], ALU.mult, ALU.mult,
                    r=[("ps", b), ("rs", 1), "qkcol"], w=[("tmp", t0)])
                if sample:
                    cp("dve", kTs[:, p, 0:N], tmp[:, t0, 0:N], r=[("tmp", t0)], w=[("kTs", p)])
                else:
                    for hh in range(2):
                        dma("pool", kscr[2 * p + hh, :, tok0:tok0 + N], tmp[64 * hh:64 * hh + 64, t0, 0:N], f"kst{t0}_{hh}",
                            r=[("tmp", t0)], w=[("kscr", 2 * p + hh)])
                st8[("t", i)] = t0

        def stageC(i):
            which, slot, p, o = tasks[i]
            if which != "k":
                return
            t0 = st8[("t", i)]
            ko = 0
            bt = tbank()
            nblk = (N + 127) // 128
            for jb in range(nblk):
                nb = min(128, N - jb * 128)
                tr(ps[bt][0:nb, jb * 128:jb * 128 + 128], tmp[:, t0, jb * 128:jb * 128 + nb], identf, r=[("tmp", t0), "cstf"], w=[("ps", bt)])
            nb = min(128, N)
            cp("dve", kost[0:nb, ko, 0:nblk, :], ps[bt][0:nb, 0:nblk * 128].rearrange("p (j c) -> p j c", c=128),
               r=[("ps", bt)], w=[("kost", ko)])
            final_refs.append(dma("pool", o_k[tok0:tok0 + N, p * 128:(p + 1) * 128].rearrange("(j t) c -> t j c", t=nb),
                                  kost[0:nb, ko, 0:nblk, :], f"o_k{ko}", r=[("kost", ko)], w=[("o_k", ti, p)]))

        stageA(0)
        for i in range(len(tasks)):
            if i + 1 < len(tasks):
                stageA(i + 1)
            stageB(i)
            if i >= 1:
                stageC(i - 1)
        stageC(len(tasks) - 1)
        chk('a_qk')
        sv = [wload("inc", 4), wload("inc", 5)]
        nb = SL if sample else 128
        for jb in range(N // nb):
            st = rot("stg", 2)
            for half in range(2):
                s, W = sv[half]
                b = mmbank()
                for k in range(8):
                    mm(ps[b][0:nb, 0:512], H[:, k, jb * nb:(jb + 1) * nb], wr(s, W, k, 0, 512), start=(k == 0), stop=(k == 7),
                       r=[("wr", s), ("H", k)], w=[("ps", b)])
                if "vnoact" not in os.environ.get("KDBG", ""):
                    act(stg[0:nb, st, 512 * half:512 * half + 512], ps[b][0:nb, :], AF.Copy, r=[("ps", b)], w=[("stg", st)])
                if "vnodve" in os.environ.get("KDBG", ""):
                    pass
                elif sample:
                    cp("dve", Vnew[0:nb, jb, 512 * half:512 * half + 512], stg[0:nb, st, 512 * half:512 * half + 512], r=[("stg", st)], w=VK(14, 4))
                else:
                    cp("dve", V[0:nb, tok0 // 128 + jb, 512 * half:512 * half + 512], stg[0:nb, st, 512 * half:512 * half + 512],
                       r=[("stg", st)], w=[("V", tok0 // 128 + jb)])
            if "ovsp" in os.environ.get("KDBG", ""):
                final_refs.append(dma("sp", o_v[tok0 + jb * nb:tok0 + (jb + 1) * nb, :], stg[0:nb, st, :], f"stg{st}", r=[("stg", st)], w=[("o_v", ti, jb)]))
            elif "noov" in os.environ.get("KDBG", ""):
                pass
            else:
                final_refs.append(dma("pool", o_v[tok0 + jb * nb:tok0 + (jb + 1) * nb, :], stg[0:nb, st, :], f"so{st}", r=[("stg", st)], w=[("o_v", ti, jb)]))
        chk('a_v')
        s, W = wload("inc", 6)
        b = mmbank()
        for k in range(8):
            mm(ps[b][0:NH, 0:N], wr(s, W, k, 0, NH), H[:, k, 0:N], start=(k == 0), stop=(k == 7), r=[("wr", s), ("H", k)], w=[("ps", b)])
        act(nl[:, 0:N], ps[b][0:NH, 0:N], AF.Exp, scale=-1.0, bias=negb[:, 0:1], r=[("ps", b), "negb"], w=["nl"])
        act(nl[:, 0:N], nl[:, 0:N], AF.Ln, bias=onec[0:NH, 0:1], r=["nl", "onec"], w=["nl"])
        if sample:
            for q in range(NS):
                P.add("dve", lambda e, q=q: e.tensor_tensor_scan(out=nd[:, q * SL:(q + 1) * SL], data0=onesf[:, 0:SL], data1=nl[:, q * SL:(q + 1) * SL],
                                                               initial=0.0, op0=ALU.mult, op1=ALU.add), ["nl", "onesf"], ["nd"])
        else:
            P.add("dve", lambda e: e.tensor_tensor_scan(out=nd[:, 0:N], data0=onesf[:, 0:N], data1=nl[:, 0:N], initial=ndcarry[:, 0:1],
                                                      op0=ALU.mult, op1=ALU.add), ["nl", "onesf", "ndcarry"], ["nd"])
            cp("dve", ndcarry[:, :], nd[:, N - 1:N], r=["nd"], w=["ndcarry"])
        chk('a_f1')
        ts("dve", spl[:, 0, 0:N], nd[:, 0:N], -1.0, None, ALU.mult, r=["nd"], w=[("spl", 0)])
        stt(r1[:, 0:N], nd[:, 0:N], -1.0, spl[:, 0, 0:N], ALU.mult, ALU.subtract, r=["nd", ("spl", 0)], w=[("rs", 0)])
        cp("dve", spl[:, 1, 0:N], r1[:, 0:N], r=[("rs", 0)], w=[("spl", 1)])
        tt("dve", r1[:, 0:N], r1[:, 0:N], spl[:, 1, 0:N], ALU.subtract, r=[("rs", 0), ("spl", 1)], w=[("rs", 0)])
        cp("dve", spl[:, 2, 0:N], r1[:, 0:N], r=[("rs", 0)], w=[("spl", 2)])
        for h in range(NH):
            base = 0 if h % 2 else 64
            for r_ in range(3):
                dma("pool", Gq[base + r_:base + r_ + 1, h, 0:N], spl[h:h + 1, r_, 0:N], f"aug{h}_{r_}", r=[("spl", r_), ("G", h)], w=[("Gaug", h, r_)])
        chk('a_f2')
        if sample:
            for q in range(NS):
                bt = tbank()
                tr(ps[bt][0:SL, 0:NH], nl[:, q * SL:(q + 1) * SL], identf[0:NH, 0:NH], r=["nl", "cstf"], w=[("ps", bt)])
                tr(ps[bt][0:SL, NH:2 * NH], nd[:, q * SL:(q + 1) * SL], identf[0:NH, 0:NH], r=["nd", "cstf"], w=[("ps", bt)])
                ts("dve", lfst[0:SL, q, :], ps[bt][0:SL, 0:NH], -1.0, None, ALU.mult, r=[("ps", bt)], w=["lfst"])
                cp("dve", ndtoks[:, q, :], ps[bt][0:SL, NH:2 * NH], r=[("ps", bt)], w=["ndtoks"])
            final_refs.append(dma("pool", o_lf.rearrange("(q t) h -> t q h", t=SL), lfst[0:SL, 0:NS, :], "o_lf", r=["lfst"], w=[("o_lf", ti)]))
        else:
            for jb in range(N // 128):
                bt = tbank()
                tr(ps[bt][:, 0:NH], nl[:, jb * 128:(jb + 1) * 128], identf[0:NH, 0:NH], r=["nl", "cstf"], w=[("ps", bt)])
                tr(ps[bt][:, NH:2 * NH], nd[:, jb * 128:(jb + 1) * 128], identf[0:NH, 0:NH], r=["nd", "cstf"], w=[("ps", bt)])
                ts("dve", lfst[:, jb, :], ps[bt][:, 0:NH], -1.0, None, ALU.mult, r=[("ps", bt)], w=["lfst"])
                cp("dve", ndtok[:, tok0 // 128 + jb, :], ps[bt][:, NH:2 * NH], r=[("ps", bt)], w=[("ndtok", tok0 // 128 + jb)])
            final_refs.append(dma("pool", o_lf[tok0:tok0 + N, :].rearrange("(j t) h -> t j h", t=128), lfst[:, 0:N // 128, :], "o_lf",
                                  r=["lfst"], w=[("o_lf", ti)]))

        chk('a_f3')
        def head_blocks(h, keys, nq, qcol0, bo, bd):
            odd = h % 2
            p = h // 2
            rw = slice(64, 128) if odd else slice(0, 64)
            nkb = len(keys)
            qk = [("G", h)] + [("Gaug", h, r_) for r_ in range(3)]
            pend = []

            def pv(kb, eslot, c0, lv, vk, nbk):
                et, eo, ekey = eslot
                mm(ps[bo][:, c0:nq], lv, E[0:nbk, et, eo + c0:eo + nq], start=(kb == 0), stop=(kb == nkb - 1), r=[ekey] + vk, w=[("ps", bo)])
                mm(ps[bd][:, c0:nq], onesb[0:nbk, :], E[0:nbk, et, eo + c0:eo + nq], start=(kb == 0), stop=(kb == nkb - 1), r=[ekey, "onesb"], w=[("ps", bd)])

            small = nq <= 16
            la = 3 if small else 2
            for kb, (lk, kk, bias, bk, lv, vk, dg, nbk) in enumerate(keys):
                c0 = 0 if dg is None else dg
                if small:
                    bs_, so = (0, 1, 6, 7)[rot("ss", 4)], 0
                    skey = ("ps", bs_)
                    el = rot("es", 12)
                    et, eo = el % 3, 16 * (el // 3)
                    ekey = ("Es", et, eo)
                else:
                    bs_, so = (0, 1, 6)[rot("sb3", 3)], 0
                    skey = ("ps", bs_)
                    et, eo = rot("E", 3), 0
                    ekey = ("E", et)
                mm(ps[bs_][0:nbk, so + c0:so + nq], lk, Gq[:, h, qcol0 + c0:qcol0 + nq], start=True, stop=(dg is None), r=qk + kk, w=[skey])
                if dg is not None:
                    dgw = min(128, nq - c0)
                    mm(ps[bs_][0:nbk, so + c0:so + c0 + dgw], identb[0:nbk, 0:nbk], negmaskb[0:nbk, 0:dgw], start=False, stop=True,
                       r=["identb", "negmaskb"], w=[skey])
                act(E[0:nbk, et, eo + c0:eo + nq], ps[bs_][0:nbk, so + c0:so + nq], AF.Exp, bias=bias, r=[skey] + bk, w=[ekey])
                if len(pend) >= la:
                    pv(*pend.pop(0))
                pend.append((kb, (et, eo, ekey), c0, lv, vk, nbk))
            while len(pend) > 1:
                pv(*pend.pop(0))
            pv(*pend.pop())
            ri = rot("tmp", 4)
            if small:
                act(tmp[rw, ri, 0:nq], ps[bd][rw, 0:nq], AF.Ln, r=[("ps", bd)], w=[("tmp", ri)])
                act(tmp[rw, ri, 0:nq], tmp[rw, ri, 0:nq], AF.Exp, scale=-1.0, r=[("tmp", ri)], w=[("tmp", ri)])
            else:
                recip(tmp[rw, ri, 0:nq], ps[bd][rw, 0:nq], r=[("ps", bd)], w=[("tmp", ri)])
            tt("dve", oT[rw, p, qcol0:qcol0 + nq], ps[bo][rw, 0:nq], tmp[rw, ri, 0:nq], ALU.mult, r=[("ps", bo), ("tmp", ri)], w=[("H", p)])

        if not sample:
            nk = tok0 + N
            for h in range(NH):
                odd = h % 2
                p = h // 2
                rw = slice(64, 128) if odd else slice(0, 64)
                dma("sp", kbuf[odd][rw, 0:nk], kscr[h, :, 0:nk], f"kb{odd}", r=[("kscr", h)], w=[("kb", odd)])
                keys = []
                for kb in range(nk // 128):
                    j = kb - tok0 // 128
                    keys.append((kbuf[odd][:, kb * 128:(kb + 1) * 128], [("kb", odd)], ndtok[:, kb, h:h + 1], [("ndtok", kb)],
                                 V[:, kb, p * 128:(p + 1) * 128], [("V", kb)], (j * 128 if j >= 0 else None), 128))
                bo, bd = (2, 3) if h % 2 == 0 else (4, 5)
                head_blocks(h, keys, N, 0, bo, bd)
        else:
            for q in range(NS):
                dma("sp", ltok[:, 0:NPB, :], clf[q, :, :].rearrange("(k s) h -> s k h", s=128), "ltok", w=VK(13, 1))
                for g4 in range((NPB + 3) // 4):
                    bt = tbank()
                    n4 = min(4, NPB - 4 * g4)
                    for kk in range(n4):
                        tr(ps[bt][0:NH, kk * 128:(kk + 1) * 128], ltok[:, 4 * g4 + kk, :], identf, r=VK(13, 1) + ["cstf"], w=[("ps", bt)])
                    ts("dve", nlc[0:NH, 512 * g4:512 * g4 + 128 * n4], ps[bt][0:NH, 0:128 * n4], -1.0, None, ALU.mult, r=[("ps", bt)], w=VK(8, 4))
                for c0 in range(0, PAST, TN):
                    cw = min(TN, PAST - c0)
                    P.add("dve", lambda e, c0=c0, cw=cw: e.tensor_tensor_scan(out=nlc[0:NH, c0:c0 + cw], data0=onesf[:, 0:cw], data1=nlc[0:NH, c0:c0 + cw],
                                                                           initial=(0.0 if c0 == 0 else nlc[0:NH, c0 - 1:c0]), op0=ALU.mult, op1=ALU.add),
                          VK(8, 4) + ["onesf"], VK(8, 4))
                cp("dve", r1[:, 0:1], nlc[0:NH, PAST - 1:PAST], r=VK(8, 4), w=[("rs", 0)])
                ts("dve", nlc[0:NH, 0:PAST], nlc[0:NH, 0:PAST], r1[:, 0:1], None, ALU.subtract, r=VK(8, 4) + [("rs", 0)], w=VK(8, 4))
                bt = tbank()
                for kb in range(NPB):
                    tr(ps[bt][:, kb * NH:(kb + 1) * NH], nlc[0:NH, kb * 128:(kb + 1) * 128], identf[0:NH, 0:NH], r=VK(8, 4) + ["cstf"], w=[("ps", bt)])
                cp("dve", biasc[:, 0:NPB, :], ps[bt][:, 0:NPB * NH].rearrange("p (k h) -> p k h", h=NH), r=[("ps", bt)], w=VK(12, 1))
                for p in range(8):
                    ci = p % 2
                    dma("pool", kc[ci][:, 0:NPB, :], ck[q, :, p * 128:(p + 1) * 128].rearrange("(k s) c -> s k c", s=128), f"kc{ci}", w=VK(2 * ci, 2))
                    dma("pool", vc[ci][:, 0:NPB, :], cv[q, :, p * 128:(p + 1) * 128].rearrange("(k s) c -> s k c", s=128), f"vc{ci}", w=VK(4 + 2 * ci, 2))
                    for g8 in range((NPB + 7) // 8):
                        n8 = min(8, NPB - 8 * g8)
                        bt = tbank()
                        pbf = ps[bt][:, :].bitcast(BF16)
                        for kk in range(n8):
                            tr(pbf[:, kk * 128:(kk + 1) * 128], kc[ci][:, 8 * g8 + kk, :], identb[:, :], r=VK(2 * ci, 2) + ["identb"], w=[("ps", bt)])
                        cp("dve", kbuf[0][0:64, 1024 * g8:1024 * g8 + 128 * n8], pbf[0:64, 0:128 * n8], r=[("ps", bt)], w=[("kb", 0)])
                        cp("dve", kbuf[1][64:128, 1024 * g8:1024 * g8 + 128 * n8], pbf[64:128, 0:128 * n8], r=[("ps", bt)], w=[("kb", 1)])
                    cp("dve", kbuf[0][0:64, PAST:PAST + SL], kTs[0:64, p, q * SL:(q + 1) * SL], r=[("kTs", p)], w=[("kb", 0)])
                    cp("dve", kbuf[1][64:128, PAST:PAST + SL], kTs[64:128, p, q * SL:(q + 1) * SL], r=[("kTs", p)], w=[("kb", 1)])
                    for hh in range(2):
                        h = 2 * p + hh
                        keys = []
                        for kb in range(NPB):
                            keys.append((kbuf[hh][:, kb * 128:(kb + 1) * 128], [("kb", hh)], biasc[:, kb, h:h + 1], VK(12, 1),
                                         vc[ci][:, kb, :], VK(4 + 2 * ci, 2), None, 128))
                        keys.append((kbuf[hh][:, PAST:PAST + SL], [("kb", hh)], ndtoks[:, q, h:h + 1], ["ndtoks"],
                                     Vnew[0:SL, q, p * 128:(p + 1) * 128], VK(14, 4), 0, SL))
                        bo, bd = (2, 3) if hh == 0 else (4, 5)
                        head_blocks(h, keys, SL, q * SL, bo, bd)
        chk('a_core')
        for half in range(2):
            s, W = wload("outc", half)
            for o in range(4):
                b = mmbank()
                proj_fm(s, W, o * 128, N, lambda k: oT[:, k, 0:N], lambda k: ("H", k), b)
                resid(half * 4 + o, b, N)

    def store_y(dst, N, ti):
        for b in range((N + 127) // 128):
            nb = min(128, N - b * 128)
            st = rot("stg", 2)
            for g in range(2):
                bank = tbank()
                for cc in range(4):
                    c = 4 * g + cc
                    tr(ps[bank][0:nb, cc * 128:(cc + 1) * 128], xT[:, c, b * 128:b * 128 + nb], identf, r=[("xT", c), "cstf"], w=[("ps", bank)])
                act(stg[0:nb, st, 512 * g:512 * g + 512], ps[bank][0:nb, :], AF.Copy, r=[("ps", bank)], w=[("stg", st)])
            final_refs.append(dma("pool", dst[b * 128:b * 128 + nb, :], stg[0:nb, st, :], f"so{st}", r=[("stg", st)], w=[("o_y", ti, b)]))

    def chk(name):
        if stop == name:
            raise _Stop()

    def state_out(src4, nrow, dst, keys, tag):
        nch = src4.shape[1]
        for g in range((nch + 3) // 4):
            n4 = min(4, nch - 4 * g)
            bank = tbank()
            st = rot("stg", 2)
            for cc in range(n4):
                tr(ps[bank][0:nrow, cc * 128:(cc + 1) * 128], src4[:, 4 * g + cc, :, :].rearrange("p s r -> p (s r)"), identf, r=list(keys) + ["cstf"], w=[("ps", bank)])
            act(stg[0:nrow, st, 0:128 * n4], ps[bank][0:nrow, 0:128 * n4], AF.Copy, r=[("ps", bank)], w=[("stg", st)])
            final_refs.append(dma("pool", dst[:, 512 * g:512 * g + 128 * n4], stg[0:nrow, st, 0:128 * n4], f"so{st}", r=[("stg", st)], w=[(tag, g)]))

    try:
        chk("init")
        for ti in range(NT):
            load_x(xp[ti * TN:(ti + 1) * TN, :], TN)
            chk("load")
            mixer_ab(False, TN, 1, TN)
            chk("mixer")
            ffn(0, False, TN, 1, TN)
            chk("ffn0")
            attn(False, ti, TN)
            chk("attn")
            ffn(1, False, TN, 1, TN)
            store_y(o_yp[ti * TN:(ti + 1) * TN, :], TN, ti)
            chk("tile")
        load_x(xs, NSTOK)
        mixer_ab(True, NSTOK, NS, SL)
        chk("smixer")
        ffn(0, True, NSTOK, NS, SL)
        chk("sffn0")
        attn(True, NT, NSTOK)
        chk("sattn")
        ffn(1, True, NSTOK, NS, SL)
        store_y(o_ys, NSTOK, NT)
        chk("stile")
        state_out(pbhP[:, :, :, :], 2, o_cbp, ["pbhP"], "o_cbp")
        state_out(pbhS[:, :, :, :], 2 * NS, o_cbs, ["pbhS"], "o_cbs")
        for l in range(2):
            state_out(zhP[:, l, :, :, :], 2, o_ffp[l, :, :], [("zhP", l)], f"o_ffp{l}")
            state_out(zhS[:, l, :, :, :], 2 * NS, o_ffs[l, :, :], [("zhS", l)], f"o_ffs{l}")
    except _Stop:
        if "nostore" not in os.environ.get("KDBG", ""):
            store_y(o_yp[0:TN, :], TN, 99)
    P.add("sp", lambda e: None, extra=final_refs)
    P.emit(nc, es)
    es.close()
    return nc


def make_consts():
    c = np.zeros((128, 384), np.float32)
    c[:, 0:128] = np.eye(128, dtype=np.float32)
    s = np.arange(128)[:, None]
    t = np.arange(128)[None, :]
    c[:, 128:256] = np.where(s <= t, 0.0, NEGM).astype(np.float32)
    c[:, 256:384] = 1.0
    return c


def core_inputs(inp, i, NS, n_cores):
    a = lambda x: np.ascontiguousarray(np.asarray(x, dtype=np.float32))
    sl = slice(NS * i, NS * (i + 1))
    m = {
        "xp": a(inp["x_prompt"][i]), "xs": a(inp["x_sample"][sl]).reshape(NS * SL, D),
        "st_cb": a(inp["state_conv_b"][0, sl]).reshape(NS * 2, 512),
        "st_ffn": a(inp["state_ffn"][:, sl]).reshape(2, NS * 2, DUP),
        "ck": a(inp["cache_k"][0, sl]).reshape(NS, -1, D), "cv": a(inp["cache_v"][0, sl]).reshape(NS, -1, D),
        "clf": a(inp["cache_logf"][0, sl]),
        "norm_mix": a(inp["norm_mix"]), "norm_ffn": a(inp["norm_ffn"]), "w_in_ab": a(inp["w_in_ab"][0]),
        "sgu_norm": a(inp["sgu_norm"][0]), "w_spatial": a(inp["w_spatial"][0]), "b_spatial": a(inp["b_spatial"][0]),
        "conv_b": a(inp["conv_b"][0]), "w_out_ab": a(inp["w_out_ab"][0]), "w_in_c": a(inp["w_in_c"][0]),
        "b_forget": a(inp["b_forget"][0]).reshape(NH, 1), "q_norm": a(inp["q_norm"][0]).reshape(64, 1),
        "k_norm": a(inp["k_norm"][0]).reshape(64, 1), "w_out_c": a(inp["w_out_c"][0]), "w_up": a(inp["w_up"]),
        "conv_ffn": a(inp["conv_ffn"]), "w_down": a(inp["w_down"]), "cst": make_consts(),
    }
    return m


def assemble(results, B, T, NS):
    cat = lambda k: np.stack([np.asarray(r[k], dtype=np.float32) for r in results])
    DB = B * NS
    yp = cat("o_yp")
    ys = cat("o_ys").reshape(DB, SL, D)
    cbp = cat("o_cbp")[None]
    cbs = cat("o_cbs").reshape(1, DB, 2, 512)
    gvs = cat("o_gv").reshape(1, DB, SL, 512)
    kp = cat("o_kp").reshape(1, B, T, NH, 64)
    vp = cat("o_vp").reshape(1, B, T, NH, 64)
    lfp = cat("o_lfp")[None]
    ks = cat("o_ks").reshape(1, DB, SL, NH, 64)
    vs = cat("o_vs").reshape(1, DB, SL, NH, 64)
    lfs = cat("o_lfs").reshape(1, DB, SL, NH)
    ffp = np.ascontiguousarray(cat("o_ffp").transpose(1, 0, 2, 3))
    ffs = np.ascontiguousarray(cat("o_ffs").reshape(B, 2, NS, 2, DUP).transpose(1, 0, 2, 3, 4)).reshape(2, DB, 2, DUP)
    return (yp, ys, cbp, cbs, gvs, kp, vp, lfp, ks, vs, lfs, ffp, ffs)


def kernel(**inputs):
    B, T, _ = inputs["x_prompt"].shape
    DB = inputs["x_sample"].shape[0]
    NS = DB // B
    PAST = inputs["cache_k"].shape[2]
    nc = build(T, NS, PAST)
    in_maps = [core_inputs(inputs, i, NS, B) for i in range(B)]
    res = run_bass_kernel_spmd(nc, in_maps, core_ids=list(range(B)))
    return assemble(res.results, B, T, NS)
```
